# Optimizing a Trainium2 kernel written in Bass

```python
import jax, jax.numpy as jnp
from jax import lax
import numpy as np


D_MODEL = 1024
BATCH = 4
SEQ = 8192
DEPTH = 2
DEC_BATCH = 8
DEC_SEQ = 8192
PAST_LEN = 128

N_MIXERS = 2
N_LAYERS_A = (DEPTH + 1) // 2
N_LAYERS_B = DEPTH // 2
HEAD_DIM = 64
MIX_WIDTH = D_MODEL
MEM_HEADS = 4
MEM_DIM = MEM_HEADS * HEAD_DIM
N_MEM = 256
NA_HEADS = (MIX_WIDTH - MEM_DIM) // HEAD_DIM
NA_DIM = NA_HEADS * HEAD_DIM
FOURIER_DIM = MIX_WIDTH - MEM_DIM
GRID_W = 64
KERNEL_ROWS = 8
KERNEL_COLS = 16
D_FF = ((8 * D_MODEL // 3 + 127) // 128) * 128
EPS = 1e-6
NEG_INF = -1e30

kernel_name = 'hybrid_natten_fnet_memory_encoder'


def _rmsnorm(x, g):
    xf = x.astype(jnp.float32)
    y = xf * lax.rsqrt(jnp.mean(xf * xf, axis=-1, keepdims=True) + EPS)
    return (y * g.astype(jnp.float32)).astype(x.dtype)


def _swiglu(h, w_in, w_out):
    g, u = jnp.split(h @ w_in, 2, axis=-1)
    return (jax.nn.silu(g) * u) @ w_out


def _memory_attention(qm, mem_h, w_kv, gq, gk):
    B, S, _ = qm.shape
    M = mem_h.shape[1]
    k, v = jnp.split(mem_h @ w_kv, 2, axis=-1)
    q = _rmsnorm(qm.reshape(B, S, MEM_HEADS, HEAD_DIM), gq)
    k = _rmsnorm(k.reshape(B, M, MEM_HEADS, HEAD_DIM), gk)
    v = v.reshape(B, M, MEM_HEADS, HEAD_DIM)
    s = jnp.einsum('bshd,bmhd->bhsm', q, k, preferred_element_type=jnp.float32) * (HEAD_DIM ** -0.5)
    p = jax.nn.softmax(s, axis=-1).astype(v.dtype)
    o = jnp.einsum('bhsm,bmhd->bshd', p, v)
    return o.reshape(B, S, MEM_DIM)


def _neighbourhood_attention_seq(q, k, v, rpb):
    S = q.shape[0]
    rows = S // GRID_W
    kr = min(KERNEL_ROWS, rows)
    q = q.reshape(rows, GRID_W, NA_HEADS, HEAD_DIM)
    k = k.reshape(rows, GRID_W, NA_HEADS, HEAD_DIM)
    v = v.reshape(rows, GRID_W, NA_HEADS, HEAD_DIM)
    r = jnp.arange(rows)
    rs = jnp.clip(r - kr // 2, 0, rows - kr)
    row_idx = rs[:, None] + jnp.arange(kr)[None, :]
    k_blk = k[row_idx]
    v_blk = v[row_idx]
    s = jnp.einsum('rqhd,rowhd->rhqow', q, k_blk, preferred_element_type=jnp.float32) * (HEAD_DIM ** -0.5)
    c = jnp.arange(GRID_W)
    cs = jnp.clip(c - KERNEL_COLS // 2, 0, GRID_W - KERNEL_COLS)
    inside = (c[None, :] >= cs[:, None]) & (c[None, :] < cs[:, None] + KERNEL_COLS)
    dr = row_idx - r[:, None] + (KERNEL_ROWS - 1)
    dc = jnp.clip(c[None, :] - c[:, None], -(KERNEL_COLS - 1), KERNEL_COLS - 1) + (KERNEL_COLS - 1)
    bias = rpb.astype(jnp.float32)[:, dr]
    bias = jnp.take(bias, dc, axis=3)
    bias = bias.transpose(1, 0, 3, 2, 4)
    s = jnp.where(inside[None, None, :, None, :], s + bias, NEG_INF)
    p = jax.nn.softmax(s.reshape(rows, NA_HEADS, GRID_W, kr * GRID_W), axis=-1)
    p = p.reshape(rows, NA_HEADS, GRID_W, kr, GRID_W).astype(v.dtype)
    o = jnp.einsum('rhqow,rowhd->rqhd', p, v_blk)
    return o.reshape(S, NA_DIM)


def _mixer_neighbourhood(h, mem_h, w_in, gq, gk, rpb, w_out, w_kv, mgq, mgk):
    B, S, _ = h.shape
    proj = h @ w_in
    q, k, v, qm = jnp.split(proj, [NA_DIM, 2 * NA_DIM, 3 * NA_DIM], axis=-1)
    q = _rmsnorm(q.reshape(B, S, NA_HEADS, HEAD_DIM), gq)
    k = _rmsnorm(k.reshape(B, S, NA_HEADS, HEAD_DIM), gk)
    v = v.reshape(B, S, NA_HEADS, HEAD_DIM)
    na = lax.map(lambda t: _neighbourhood_attention_seq(t[0], t[1], t[2], rpb), (q, k, v))
    mo = _memory_attention(qm, mem_h, w_kv, mgq, mgk)
    return jnp.concatenate([na, mo], axis=-1) @ w_out


def _mixer_fourier(h, mem_h, w_in, w_out, w_kv, mgq, mgk):
    proj = h @ w_in
    z, qm = jnp.split(proj, [FOURIER_DIM], axis=-1)
    f = jnp.fft.fft2(z.astype(jnp.float32), axes=(1, 2), norm='ortho').real.astype(h.dtype)
    mo = _memory_attention(qm, mem_h, w_kv, mgq, mgk)
    return jnp.concatenate([f, mo], axis=-1) @ w_out


def _trunk(x, mem, norm_ffn1, w_ffn1_in, w_ffn1_out, norm_mix, norm_mem, w_mem_kv, mem_q_norm, mem_k_norm,
           w_in_a, na_q_norm, na_k_norm, na_rpb, w_out_a, w_in_b, w_out_b,
           norm_ffn2, w_ffn2_in, w_ffn2_out, norm_out):
    for i in range(DEPTH):
        x = x + 0.5 * _swiglu(_rmsnorm(x, norm_ffn1[i]), w_ffn1_in[i], w_ffn1_out[i])
        h = _rmsnorm(x, norm_mix[i])
        mem_h = _rmsnorm(mem, norm_mem[i])
        j = i // N_MIXERS
        if i % N_MIXERS == 0:
            x = x + _mixer_neighbourhood(h, mem_h, w_in_a[j], na_q_norm[j], na_k_norm[j], na_rpb[j], w_out_a[j],
                                         w_mem_kv[i], mem_q_norm[i], mem_k_norm[i])
        else:
            x = x + _mixer_fourier(h, mem_h, w_in_b[j], w_out_b[j], w_mem_kv[i], mem_q_norm[i], mem_k_norm[i])
        x = x + 0.5 * _swiglu(_rmsnorm(x, norm_ffn2[i]), w_ffn2_in[i], w_ffn2_out[i])
        x = _rmsnorm(x, norm_out[i])
    return x


def setup_inputs(seed: int = 0) -> dict:
    key = jax.random.key(seed)
    ks = jax.random.split(key, 24)
    f32 = jnp.float32

    def nrm(k, shape, scale):
        return jax.random.normal(k, shape, f32) * scale

    def gain(k, shape):
        return 1.0 + 0.01 * jax.random.normal(k, shape, f32)

    D = D_MODEL
    return {
        'x_prompt': nrm(ks[0], (BATCH, SEQ, D), 1.0),
        'x_sample': nrm(ks[1], (DEC_BATCH, DEC_SEQ, D), 1.0),
        'mem_prompt': nrm(ks[2], (BATCH, N_MEM, D), 1.0),
        'mem_sample': nrm(ks[3], (DEC_BATCH, N_MEM, D), 1.0),
        'norm_ffn1': gain(ks[4], (DEPTH, D)),
        'w_ffn1_in': nrm(ks[5], (DEPTH, D, 2 * D_FF), D ** -0.5),
        'w_ffn1_out': nrm(ks[6], (DEPTH, D_FF, D), D_FF ** -0.5),
        'norm_mix': gain(ks[7], (DEPTH, D)),
        'norm_mem': gain(ks[8], (DEPTH, D)),
        'w_mem_kv': nrm(ks[9], (DEPTH, D, 2 * MEM_DIM), D ** -0.5),
        'mem_q_norm': gain(ks[10], (DEPTH, HEAD_DIM)),
        'mem_k_norm': gain(ks[11], (DEPTH, HEAD_DIM)),
        'w_in_a': nrm(ks[12], (N_LAYERS_A, D, 3 * NA_DIM + MEM_DIM), D ** -0.5),
        'na_q_norm': gain(ks[13], (N_LAYERS_A, HEAD_DIM)),
        'na_k_norm': gain(ks[14], (N_LAYERS_A, HEAD_DIM)),
        'na_rpb': nrm(ks[15], (N_LAYERS_A, NA_HEADS, 2 * KERNEL_ROWS - 1, 2 * KERNEL_COLS - 1), 0.1),
        'w_out_a': nrm(ks[16], (N_LAYERS_A, MIX_WIDTH, D), MIX_WIDTH ** -0.5),
        'w_in_b': nrm(ks[17], (N_LAYERS_B, D, FOURIER_DIM + MEM_DIM), D ** -0.5),
        'w_out_b': nrm(ks[18], (N_LAYERS_B, MIX_WIDTH, D), MIX_WIDTH ** -0.5),
        'norm_ffn2': gain(ks[19], (DEPTH, D)),
        'w_ffn2_in': nrm(ks[20], (DEPTH, D, 2 * D_FF), D ** -0.5),
        'w_ffn2_out': nrm(ks[21], (DEPTH, D_FF, D), D_FF ** -0.5),
        'norm_out': gain(ks[22], (DEPTH, D)),
    }


def reference(x_prompt, x_sample, mem_prompt, mem_sample, norm_ffn1, w_ffn1_in, w_ffn1_out, norm_mix, norm_mem,
              w_mem_kv, mem_q_norm, mem_k_norm, w_in_a, na_q_norm, na_k_norm, na_rpb, w_out_a, w_in_b, w_out_b,
              norm_ffn2, w_ffn2_in, w_ffn2_out, norm_out):
    y_prompt = _trunk(x_prompt, mem_prompt, norm_ffn1, w_ffn1_in, w_ffn1_out, norm_mix, norm_mem, w_mem_kv,
                      mem_q_norm, mem_k_norm, w_in_a, na_q_norm, na_k_norm, na_rpb, w_out_a, w_in_b, w_out_b,
                      norm_ffn2, w_ffn2_in, w_ffn2_out, norm_out)
    y_sample = _trunk(x_sample, mem_sample, norm_ffn1, w_ffn1_in, w_ffn1_out, norm_mix, norm_mem, w_mem_kv,
                      mem_q_norm, mem_k_norm, w_in_a, na_q_norm, na_k_norm, na_rpb, w_out_a, w_in_b, w_out_b,
                      norm_ffn2, w_ffn2_in, w_ffn2_out, norm_out)
    return (y_prompt, y_sample)
```

```python
import contextlib
import types
import numpy as np
import ml_dtypes
import concourse.bass as bass
import concourse.mybir as mybir
from concourse.bass_utils import run_bass_kernel_spmd

F32 = mybir.dt.float32
BF16 = mybir.dt.bfloat16
AF = mybir.ActivationFunctionType
ALU = mybir.AluOpType

D = 1024
SEQ = 8192
DFF = 2816
NMEM = 256
CH = 512
SB_BASE = 17408
SB_LIMIT = 229312


class Buf:
    __slots__ = ("name", "w", "r", "pw", "pr", "excl")

    def __init__(self, name="", excl=False):
        self.name = name
        self.excl = excl
        self.w = []
        self.r = []
        self.pw = []
        self.pr = []


class Sched:
    ENGS = ("pe", "act", "dve", "pool", "sp")

    def __init__(self, nc, n_dma_sems=10):
        self.nc = nc
        self.ops = {e: [] for e in self.ENGS}
        self.cnt = {e: 0 for e in self.ENGS}
        self.n_dma_sems = n_dma_sems
        self.dma_i = {e: 0 for e in ("sp", "act", "pool")}
        self.dma_last = {e: [None] * n_dma_sems for e in ("sp", "act", "pool")}
        self.bar = []
        self.bar_id = 0
        self.passed = {e: 0 for e in self.ENGS}
        self.psum = []
        self.ps_i = 0

    def _deps(self, eng, issue_eng, reads, writes, extra, add_w):
        waits = list(extra)
        if self.passed[issue_eng] != self.bar_id:
            waits.extend(self.bar)
            self.passed[issue_eng] = self.bar_id
        for b in reads:
            waits.extend(b.w)
            if b.excl:
                waits.extend(b.r)
        for b in writes:
            if add_w:
                waits.extend(b.pw)
                waits.extend(b.pr)
            else:
                waits.extend(b.w)
            waits.extend(b.r)
        return waits

    @staticmethod
    def _compact(lst):
        if len(lst) <= 16:
            return lst
        d = {}
        for t in lst:
            if t[0] not in d or d[t[0]] < t[1]:
                d[t[0]] = t[1]
        return list(d.items())

    def _commit(self, tok, reads, writes, add_w):
        for b in reads:
            b.r = self._compact(b.r + [tok])
        for b in writes:
            if add_w:
                b.w = self._compact(b.w + [tok])
            else:
                b.pw = b.w
                b.pr = b.r
                b.w = [tok]
                b.r = []

    @staticmethod
    def _freeze(fn):
        if fn.__closure__ is None:
            return fn
        cells = []
        for c_ in fn.__closure__:
            try:
                cells.append(types.CellType(c_.cell_contents))
            except ValueError:
                cells.append(c_)
        g = types.FunctionType(fn.__code__, fn.__globals__, fn.__name__, fn.__defaults__, tuple(cells))
        g.__kwdefaults__ = fn.__kwdefaults__
        return g

    @staticmethod
    def _flat(lst):
        out = []
        for b in lst:
            if isinstance(b, (list, tuple)):
                out.extend(Sched._flat(b))
            else:
                out.append(b)
        return out

    def op(self, eng, fn, reads=(), writes=(), extra=(), add_w=False):
        fn = self._freeze(fn)
        reads, writes = self._flat(reads), self._flat(writes)
        waits = self._deps(eng, eng, reads, writes, extra, add_w)
        self.cnt[eng] += 1
        tok = (eng, self.cnt[eng])
        self.ops[eng].append((waits, fn, ("self", 1)))
        self._commit(tok, reads, writes, add_w)
        return tok

    def dma(self, q, out_ap, in_ap, reads=(), writes=(), extra=(), add_w=False, **kw):
        reads, writes = self._flat(reads), self._flat(writes)
        waits = self._deps("dma_" + q, q, reads, writes, extra, add_w)
        i = self.dma_i[q]
        self.dma_i[q] += 1
        slot = i % self.n_dma_sems
        val = 16 * (i // self.n_dma_sems + 1)
        key = "dma_%s_%d" % (q, slot)
        prev = self.dma_last[q][slot]
        if prev is not None:
            waits.append(prev)
        tok = (key, val)
        self.dma_last[q][slot] = tok

        def fn(e, out_ap=out_ap, in_ap=in_ap, kw=kw):
            return e.dma_start(out=out_ap, in_=in_ap, **kw)
        self.ops[q].append((waits, fn, (key, 16)))
        self._commit(tok, reads, writes, add_w)
        return tok

    def all_tokens(self):
        toks = []
        for q in ("sp", "act", "pool"):
            for t in self.dma_last[q]:
                if t is not None:
                    toks.append(t)
        for e in self.ENGS:
            if self.cnt[e] > 0:
                toks.append((e, self.cnt[e]))
        return toks

    def barrier(self):
        self.bar = self.all_tokens()
        self.bar_id += 1

    def get_psum(self):
        ap, b = self.psum[self.ps_i % len(self.psum)]
        self.ps_i += 1
        return ap, b

    def emit(self):
        nc = self.nc
        keys = list(self.ENGS)
        for q in ("sp", "act", "pool"):
            for s in range(self.n_dma_sems):
                keys.append("dma_%s_%d" % (q, s))
        with contextlib.ExitStack() as st:
            sems = {k: st.enter_context(nc.semaphore("s_" + k)) for k in keys}
            block = st.enter_context(nc.Block())
            final_waits = self.all_tokens()

            def run(engname, engine):
                known = {}
                for waits, fn, sig in self.ops[engname]:
                    for (k, v) in waits:
                        if known.get(k, 0) < v:
                            if k == engname and v > known.get("__self_emitted", 0):
                                raise RuntimeError("self-deadlock on %s" % engname)
                            engine.wait_ge(sems[k], v)
                            known[k] = v
                    ins = fn(engine)
                    if sig[0] == "self":
                        ins.then_inc(sems[engname], 1)
                        known["__self_emitted"] = known.get("__self_emitted", 0) + 1
                    else:
                        ins.then_inc(sems[sig[0]], sig[1])
                if engname == "sp":
                    for (k, v) in final_waits:
                        if known.get(k, 0) < v:
                            engine.wait_ge(sems[k], v)
                            known[k] = v

            @block.tensor
            def _(e):
                run("pe", e)

            @block.scalar
            def _(e):
                run("act", e)

            @block.vector
            def _(e):
                run("dve", e)

            @block.gpsimd
            def _(e):
                run("pool", e)

            @block.sync
            def _(e):
                run("sp", e)


class Arena:
    def __init__(self, nc, base, tag):
        self.nc, self.off, self.tag, self.n = nc, base, tag, 0

    def alloc(self, shape, dtype, name=None):
        esz = 4 if dtype == F32 else 2
        nbytes = esz
        for s in shape[1:]:
            nbytes *= s
        nbytes = (nbytes + 63) // 64 * 64
        self.n += 1
        nm = "%s_%s_%d" % (self.tag, name or "t", self.n)
        t = self.nc.alloc_sbuf_tensor_at(nm, list(shape), dtype, offset=self.off)
        self.off += nbytes
        assert self.off <= SB_LIMIT, ("SBUF overflow", self.tag, self.off)
        return t.ap(), Buf(nm)


WSPEC = [
    ("w_ffn1_in", 2 * D, 2 * DFF), ("w_ffn1_out", 2 * DFF, D),
    ("w_ffn2_in", 2 * D, 2 * DFF), ("w_ffn2_out", 2 * DFF, D),
    ("w_mem_kv", 2 * D, 512), ("w_in_a", D, 2560), ("w_out_a", D, D),
    ("w_in_b", D, D), ("w_out_b", D, D),
]
C_FFN1, C_MIX, C_MEM, C_FFN2, C_OUT = 0, 16, 32, 48, 64
C_MQ, C_MK, C_NAQ, C_NAK = 80, 82, 84, 85
NCOL = 86
SLOT_EL = 5632
NSLOT = 4


def build_program(NSEQ, dbg=False, stop=None):
    nc = bass.Bass("TRN2", target_bir_lowering=False)
    T = NSEQ * SEQ
    NCHK = T // CH
    S = Sched(nc)

    def din(name, shape, dt=F32):
        return nc.dram_tensor(name, list(shape), dt, kind="ExternalInput").ap()

    def dscr(name, shape, dt):
        if dbg and not name.startswith("wb_"):
            return nc.dram_tensor(name, list(shape), dt, kind="ExternalOutput").ap()
        return nc.dram_tensor(name, list(shape), dt).ap()

    x_in = din("x_in", [T, D])
    mem_in = din("mem_in", [NSEQ * NMEM, D])
    cols_in = din("cols_in", [128, NCOL])
    rpb_in = din("rpb_rev", [12, 15, 31])
    grow_in = din("grow_in", [128, D])
    cd_in = din("cd_in", [768, 768], BF16)
    sd_in = din("sd_in", [768, 768], BF16)
    c128_in = din("c128_in", [128, 3, 128], BF16)
    e_in = din("e_in", [64, 2, SEQ], BF16)
    W32 = {n: din(n, [r, c]) for (n, r, c) in WSPEC}
    WB = {n: dscr("wb_" + n, [r, c], BF16) for (n, r, c) in WSPEC}
    y_out = nc.dram_tensor("y_out", [T, D], F32, kind="ExternalOutput").ap()

    XA = dscr("XA", [8, 128, T], F32)
    Q_d = dscr("Q_d", [6, 128, T], BF16)
    K_d = dscr("K_d", [6, 128, T], BF16)
    V_d = dscr("V_d", [T, 768], BF16)
    O_d = dscr("O_d", [6, 128, T], BF16)
    MO_d = dscr("MO_d", [64, 4, T], BF16)
    A_d = dscr("A_d", [NSEQ, 6, 128, 64, 128], BF16)
    B_d = dscr("B_d", [NSEQ, 6, 128, 64, 128], BF16)
    FT_d = dscr("FT_d", [6, 128, T], BF16)
    b_WB, b_XA, b_Q, b_K, b_V, b_O, b_MO, b_A, b_B, b_FT = [Buf(n) for n in
        "WB XA Q K V O MO A B FT".split()]

    for i in range(7):
        S.psum.append((nc.alloc_psum_tensor("ps%d" % i, [128, 512], F32).ap(), Buf("ps%d" % i, excl=True)))
    ssq = nc.alloc_psum_tensor("ps_ssq", [128, 512], F32).ap()
    ssq_b = Buf("ps_ssq", excl=True)

    def P(fn, r=(), w=(), **k):
        return S.op("pe", fn, reads=r, writes=w, **k)

    def A(fn, r=(), w=(), **k):
        return S.op("act", fn, reads=r, writes=w, **k)

    def V(fn, r=(), w=(), **k):
        return S.op("dve", fn, reads=r, writes=w, **k)

    def G(fn, r=(), w=(), **k):
        return S.op("pool", fn, reads=r, writes=w, **k)

    perm = Arena(nc, SB_BASE, "perm")
    cols, b_cols = perm.alloc([128, NCOL + 4], F32, "cols")
    ident, b_const = perm.alloc([128, 128], F32, "ident")
    ones_bf, _ = perm.alloc([128, 128], BF16, "ones")
    blk_bf, _ = perm.alloc([128, 128], BF16, "blk")
    epsc, _ = perm.alloc([128, 1], F32, "eps")
    kmT = {}
    Vm = {}
    for s in range(NSEQ):
        for l in range(2):
            kmT[(s, l)] = perm.alloc([128, 2, 256], BF16, "kmT")
            Vm[(s, l)] = perm.alloc([128, 2, 256], BF16, "Vm")
    PERM_END = perm.off

    S.dma("sp", cols[:, 0:NCOL], cols_in, writes=[b_cols])
    G(lambda e: e.memset(ident, 0.0), w=[b_const])
    G(lambda e: e.affine_select(out=ident, in_=ident, pattern=[[-1, 128]], compare_op=ALU.not_equal,
                                fill=1.0, base=0, channel_multiplier=1), r=[b_const], w=[b_const])
    G(lambda e: e.memset(ones_bf, 1.0), w=[b_const])
    G(lambda e: e.memset(blk_bf, 0.0), w=[b_const])
    G(lambda e: e.memset(blk_bf[0:64, 0:64], 1.0), w=[b_const])
    G(lambda e: e.memset(blk_bf[64:128, 64:128], 1.0), w=[b_const])
    G(lambda e: e.memset(epsc, 1e-6), w=[b_const])
    V(lambda e: e.tensor_scalar_mul(out=cols[:, NCOL:NCOL + 2], in0=cols[:, C_MQ:C_MQ + 2], scalar1=0.125),
      r=[b_cols], w=[b_cols])
    V(lambda e: e.tensor_scalar_mul(out=cols[:, NCOL + 2:NCOL + 3], in0=cols[:, C_NAQ:C_NAQ + 1], scalar1=0.125),
      r=[b_cols], w=[b_cols])
    C_MQS, C_NAQS = NCOL, NCOL + 2

    if stop == 'pconst':
        S.emit()
        return nc
    ar = Arena(nc, PERM_END, "p0")
    CW = 2048
    st32 = [ar.alloc([128, CW], F32, "st32") for _ in range(3)]
    st16 = [ar.alloc([128, CW], BF16, "st16") for _ in range(3)]
    it = 0
    for (n, r, c) in WSPEC:
        src = W32[n].rearrange("(p r) c -> p (r c)", p=128)
        dst = WB[n].rearrange("(p r) c -> p (r c)", p=128)
        tot = r * c // 128
        for o in range(0, tot, CW):
            wdt = min(CW, tot - o)
            a32, b32 = st32[it % 3]
            a16, b16 = st16[it % 3]
            S.dma("sp", a32[:, :wdt], src[:, o:o + wdt], writes=[b32])
            import os
            engs_ = os.environ.get("P0_ENG", "dve,act").split(",")
            eng = engs_[it % len(engs_)]
            if eng == "act":
                A(lambda e, a16=a16, a32=a32, wdt=wdt: e.copy(out=a16[:, :wdt], in_=a32[:, :wdt]), r=[b32], w=[b16])
            else:
                S.op(eng, lambda e, a16=a16, a32=a32, wdt=wdt: e.tensor_copy(out=a16[:, :wdt], in_=a32[:, :wdt]),
                     reads=[b32], writes=[b16])
            S.dma(os.environ.get("P0_STQ", "pool"), dst[:, o:o + wdt], a16[:, :wdt], reads=[b16], writes=[b_WB], add_w=True)
            it += 1
    S.barrier()

    def WL(name, l, rows_per_layer):
        return WB[name][l * rows_per_layer:(l + 1) * rows_per_layer, :]

    if stop == 'p0':
        S.emit()
        return nc
    class Ctx:
        pass

    def make_ctx(tag, nx=2, nmo=2):
        c = Ctx()
        ar = Arena(nc, PERM_END, tag)
        c.ar = ar
        c.slots = [ar.alloc([128, SLOT_EL], BF16, "slot") for _ in range(NSLOT)]
        c.slot_i = 0
        c.xT = []
        for _ in range(nx):
            xa_, _xb = ar.alloc([128, 8, CH], F32, "xT")
            c.xT.append((xa_, [Buf("xT%d" % i) for i in range(8)]))
        c.hb = ar.alloc([128, 8, CH], BF16, "hb")
        c.sq = ar.alloc([128, 8, CH], BF16, "sq")
        c.rt = [ar.alloc([128, CH], F32, "rt") for _ in range(2)]
        c.rstd = [ar.alloc([128, CH], F32, "rstd") for _ in range(2)]
        c.act = ar.alloc([128, 22, CH], BF16, "act")
        c.sg = [ar.alloc([128, CH], F32, "sg") for _ in range(2)]
        c.qf = [ar.alloc([128, CH], F32, "qf") for _ in range(3)]
        c.sqh = [ar.alloc([128, CH], BF16, "sqh") for _ in range(3)]
        c.pT = [ar.alloc([128, CH], BF16, "pT") for _ in range(4)]
        c.mo = [ar.alloc([64, 4, CH], BF16, "mo") for _ in range(nmo)]
        c.qmn = ar.alloc([128, 2, CH], BF16, "qmn")
        c.rec = [ar.alloc([64, CH], F32, "rec") for _ in range(2)]
        c.hrt = [ar.alloc([128, CH], F32, "hrt") for _ in range(2)]
        c.hrs = [ar.alloc([128, CH], F32, "hrs") for _ in range(2)]
        c.ru = [ar.alloc([128, CH], F32, "ru") for _ in range(2)]
        c.sqk = [Buf("sqk%d" % i) for i in range(8)]
        c.rs_i = 0
        c.k = 0
        return c

    def wslot(c, parts):
        ap, b = c.slots[c.slot_i % NSLOT]
        c.slot_i += 1
        first = True
        for dst_fn, src in parts:
            S.dma("sp", dst_fn(ap), src, reads=[b_WB], writes=[b], add_w=not first)
            first = False
        return ap, b

    def kx(w, c0, n, kc=8):
        return w[:, c0:c0 + n].rearrange("(k p) n -> p k n", p=128)

    def rr(c, lst):
        c.k += 1
        return lst[c.k % len(lst)]

    def rms(c, x_ap, x_b, gbase, out_ap, out_b, n=CH, nchunk=8, dim=D):
        sq, sq_b = c.sq
        A(lambda e: e.activation(out=sq[:, :nchunk, :n], in_=x_ap, func=AF.Square), r=[x_b], w=[sq_b])
        ps, pb = S.get_psum()

        def mm(e):
            ins = None
            for k in range(nchunk):
                ins = e.matmul(ps[:, :n], lhsT=ones_bf, rhs=sq[:, k, :n], start=(k == 0), stop=(k == nchunk - 1))
            return ins
        P(mm, r=[sq_b, b_const], w=[pb])
        rt, rt_b = rr(c, c.rt)
        rs, rs_b = rr(c, c.rstd)
        A(lambda e: e.activation(out=rt[:, :n], in_=ps[:, :n], func=AF.Sqrt, bias=epsc[:, 0:1], scale=1.0 / dim),
          r=[pb, b_const], w=[rt_b])
        V(lambda e: e.reciprocal(out=rs[:, :n], in_=rt[:, :n]), r=[rt_b], w=[rs_b])
        for k in range(nchunk):
            V(lambda e, k=k: e.scalar_tensor_tensor(out=out_ap[:, k, :], in0=x_ap[:, k, :],
                                                    scalar=cols[:, gbase + k:gbase + k + 1], in1=rs[:, :n],
                                                    op0=ALU.mult, op1=ALU.mult),
              r=[x_b, rs_b, b_cols], w=[out_b], add_w=(k > 0))

    def norm_hook(c, x_ap, x_b, k, gbase, n=CH, mul_on_dve=False):
        hb, hb_b = c.hb
        sq = c.sq[0]
        sqb = c.sqk[k]
        if gbase is not None and mul_on_dve:
            V(lambda e: e.tensor_scalar_mul(out=hb[:, k, :n], in0=x_ap[:, k, :n], scalar1=cols[:, gbase + k:gbase + k + 1]),
              r=[x_b[k], b_cols], w=[hb_b], add_w=(k > 0))
        elif gbase is not None:
            A(lambda e: e.mul(out=hb[:, k, :n], in_=x_ap[:, k, :n], mul=cols[:, gbase + k:gbase + k + 1]),
              r=[x_b[k], b_cols], w=[hb_b], add_w=(k > 0))
        A(lambda e: e.activation(out=sq[:, k, :n], in_=x_ap[:, k, :n], func=AF.Square), r=[x_b[k]], w=[sqb])

        def pe_part():
            P(lambda e: e.matmul(ssq[:, :n], lhsT=ones_bf, rhs=sq[:, k, :n], start=(k == 0), stop=(k == 7)),
              r=[sqb, b_const], w=[ssq_b], add_w=(k > 0))
        return pe_part

    def norm_finish(c, dim=D, n=CH):
        c.rs_i += 1
        rt, rt_b = c.rt[c.rs_i % 2]
        rs, rs_b = c.rstd[c.rs_i % 2]
        A(lambda e: e.activation(out=rt[:, :n], in_=ssq[:, :n], func=AF.Ln, bias=epsc[:, 0:1], scale=1.0 / dim),
          r=[ssq_b, b_const], w=[rt_b])
        A(lambda e: e.activation(out=rs[:, :n], in_=rt[:, :n], func=AF.Exp, scale=-0.5), r=[rt_b], w=[rs_b])
        return rs, rs_b

    def headnorm(c, ps, pb, gcol, out_ap, out_b, n=CH, add_w=False, rs=None, defer=False):
        c.h_i = getattr(c, "h_i", 0) + 1
        sqh, sqh_b = c.sqh[c.h_i % 3]
        qf, qf_b = c.qf[c.h_i % 3]
        if rs is None:
            V(lambda e: e.tensor_copy(out=qf[:, :n], in_=ps[:, :n]), r=[pb], w=[qf_b])
        else:
            V(lambda e: e.tensor_tensor(out=qf[:, :n], in0=ps[:, :n], in1=rs[0][:, :n], op=ALU.mult), r=[pb, rs[1]], w=[qf_b])
        A(lambda e: e.activation(out=sqh[:, :n], in_=qf[:, :n], func=AF.Square), r=[qf_b], w=[sqh_b])

        def part2():
            ps2, pb2 = S.get_psum()
            P(lambda e: e.matmul(ps2[:, :n], lhsT=blk_bf, rhs=sqh[:, :n], start=True, stop=True),
              r=[sqh_b, b_const], w=[pb2])
            c.h2_i = getattr(c, "h2_i", 0) + 1
            rt, rt_b = c.hrt[c.h2_i % 2]
            hs, hs_b = c.hrs[c.h2_i % 2]
            A(lambda e: e.activation(out=rt[:, :n], in_=ps2[:, :n], func=AF.Ln, bias=epsc[:, 0:1], scale=1.0 / 64),
              r=[pb2, b_const], w=[rt_b])
            A(lambda e: e.activation(out=hs[:, :n], in_=rt[:, :n], func=AF.Exp, scale=-0.5), r=[rt_b], w=[hs_b])
            V(lambda e: e.scalar_tensor_tensor(out=out_ap, in0=qf[:, :n], scalar=cols[:, gcol:gcol + 1], in1=hs[:, :n],
                                               op0=ALU.mult, op1=ALU.mult),
              r=[qf_b, hs_b, b_cols], w=[out_b], add_w=add_w)
        if defer:
            return part2
        part2()
        return None

    def ffn(c, x_ap, x_b, l, which, rs, next_g, want_next=True, mid=None):
        hb, hb_b = c.hb
        act, act_b = c.act
        rs_ap, rs_b = rs
        w_in = WL("w_ffn%d_in" % which, l, D)
        w_out = WL("w_ffn%d_out" % which, l, DFF)
        for t in range(11):
            sl, sl_b = wslot(c, [
                (lambda a: a[:, 0:4096].rearrange("p (k g n) -> p k g n", k=8, g=2)[:, :, 0, :], kx(w_in, t * 256, 256)),
                (lambda a: a[:, 0:4096].rearrange("p (k g n) -> p k g n", k=8, g=2)[:, :, 1, :], kx(w_in, DFF + t * 256, 256)),
            ])
            sv = sl[:, 0:4096].rearrange("p (k g n) -> p k g n", k=8, g=2)
            for hc in range(2):
                hidx = t * 2 + hc
                psg, pbg = S.get_psum()
                psu, pbu = S.get_psum()

                def mmg(e):
                    ins = None
                    for k in range(8):
                        ins = e.matmul(psg[:, :], lhsT=sv[:, k, 0, hc * 128:(hc + 1) * 128], rhs=hb[:, k, :],
                                       start=(k == 0), stop=(k == 7))
                    return ins

                def mmu(e):
                    ins = None
                    for k in range(8):
                        ins = e.matmul(psu[:, :], lhsT=sv[:, k, 1, hc * 128:(hc + 1) * 128], rhs=hb[:, k, :],
                                       start=(k == 0), stop=(k == 7))
                    return ins
                P(mmg, r=[sl_b, hb_b], w=[pbg])
                P(mmu, r=[sl_b, hb_b], w=[pbu])
                sg, sg_b = rr(c, c.sg)
                ru, ru_b = rr(c, c.ru)
                V(lambda e: e.tensor_tensor(out=sg, in0=psg, in1=rs_ap, op=ALU.mult), r=[pbg, rs_b], w=[sg_b])
                A(lambda e: e.activation(out=sg, in_=sg, func=AF.Silu), r=[sg_b], w=[sg_b])
                V(lambda e: e.tensor_tensor(out=ru, in0=psu, in1=rs_ap, op=ALU.mult), r=[pbu, rs_b], w=[ru_b])
                V(lambda e: e.tensor_tensor(out=act[:, hidx, :], in0=sg, in1=ru, op=ALU.mult),
                  r=[sg_b, ru_b], w=[act_b], add_w=(hidx > 0))
        if mid is not None:
            mid()
        pending = None
        for mq in range(4):
            sl, sl_b = wslot(c, [(lambda a: a[:, 0:5632].rearrange("p (k n) -> p k n", k=22),
                                  kx(w_out, mq * 256, 256))])
            sv = sl[:, 0:5632].rearrange("p (k n) -> p k n", k=22)
            for mh in range(2):
                m = mq * 2 + mh
                ps, pb = S.get_psum()

                def mmd(e):
                    ins = None
                    for k in range(22):
                        ins = e.matmul(ps[:, :], lhsT=sv[:, k, mh * 128:(mh + 1) * 128], rhs=act[:, k, :],
                                       start=(k == 0), stop=(k == 21))
                    return ins
                P(mmd, r=[sl_b, act_b], w=[pb])
                if pending is not None:
                    pending()
                    pending = None
                V(lambda e: e.scalar_tensor_tensor(out=x_ap[:, m, :], in0=ps, scalar=0.5, in1=x_ap[:, m, :],
                                                   op0=ALU.mult, op1=ALU.add),
                  r=[pb, x_b[m]], w=[x_b[m]], add_w=True)
                if want_next:
                    pending = norm_hook(c, x_ap, x_b, m, next_g)
        if pending is not None:
            pending()

    def mem_attn(c, s, l, mo_ap, mo_b):
        qmn, qmn_b = c.qmn
        km, km_b = kmT[(s, l)]
        vm, vm_b = Vm[(s, l)]
        c.pt_i = getattr(c, "pt_i", 0)

        def qk(hm):
            fc, hp = hm // 2, hm % 2
            pts = []
            for mt in range(2):
                ps, pb = S.get_psum()
                P(lambda e: e.matmul(
                    ps[:, :], lhsT=km[hp * 64:(hp + 1) * 64, fc, mt * 128:(mt + 1) * 128],
                    rhs=qmn[hp * 64:(hp + 1) * 64, fc, :], start=True, stop=True),
                  r=[km_b, qmn_b], w=[pb])
                pt, pt_b = c.pT[c.pt_i % 4]
                c.pt_i += 1
                A(lambda e: e.activation(out=pt, in_=ps, func=AF.Exp), r=[pb], w=[pt_b])
                pts.append((pt, pt_b))
            return pts

        def pv(hm, pts):
            pso, pbo = S.get_psum()
            psd, pbd = S.get_psum()

            def mmo(e):
                ins = None
                for mt in range(2):
                    ins = e.matmul(pso[0:64, :], lhsT=vm[:, mt, hm * 64:(hm + 1) * 64], rhs=pts[mt][0],
                                   start=(mt == 0), stop=(mt == 1))
                return ins

            def mmden(e):
                ins = None
                for mt in range(2):
                    ins = e.matmul(psd[0:64, :], lhsT=ones_bf[:, 0:64], rhs=pts[mt][0],
                                   start=(mt == 0), stop=(mt == 1))
                return ins
            P(mmo, r=[vm_b, pts[0][1], pts[1][1]], w=[pbo])
            P(mmden, r=[b_const, pts[0][1], pts[1][1]], w=[pbd])
            rec, rec_b = rr(c, c.rec)
            A(lambda e: e.activation(out=rec, in_=psd[0:64, :], func=AF.Ln), r=[pbd], w=[rec_b])
            A(lambda e: e.activation(out=rec, in_=rec, func=AF.Exp, scale=-1.0), r=[rec_b], w=[rec_b])
            V(lambda e: e.tensor_tensor(out=mo_ap[:, hm, :], in0=rec, in1=pso[0:64, :], op=ALU.mult),
              r=[rec_b, pbo], w=[mo_b], add_w=(hm > 0))

        prev = qk(0)
        for hm in range(4):
            nxt = qk(hm + 1) if hm + 1 < 4 else None
            pv(hm, prev)
            prev = nxt

    def mo_project(c, wname, mo_ap, mo_b, x_ap, x_b, extra_k=None, hook=False, next_g=None):
        w = WB[wname]
        slm, slm_b = wslot(c, [(lambda a: a[0:64, 0:4096].rearrange("p (h n) -> p h n", h=4),
                                w[768:1024, :].rearrange("(h p) n -> p h n", p=64))])
        svm = slm[0:64, 0:4096].rearrange("p (h n) -> p h n", h=4)
        pend = [None]
        for half in range(2):
            if extra_k is not None:
                sle, sle_b = wslot(c, [(lambda a: a[:, 0:3072].rearrange("p (k n) -> p k n", k=6),
                                        w[0:768, half * 512:(half + 1) * 512].rearrange("(k p) n -> p k n", p=128))])
                sve = sle[:, 0:3072].rearrange("p (k n) -> p k n", k=6)
            for mm_ in range(4):
                m = half * 4 + mm_
                ps, pb = S.get_psum()

                def mmf(e, ps=ps, m=m, mm_=mm_):
                    ins = None
                    first = True
                    if extra_k is not None:
                        for k in range(6):
                            ins = e.matmul(ps[:, :], lhsT=sve[:, k, mm_ * 128:(mm_ + 1) * 128], rhs=extra_k[0][:, k, :],
                                           start=first, stop=(mo_ap is None and k == 5))
                            first = False
                    if mo_ap is not None:
                        for h in range(4):
                            ins = e.matmul(ps[:, :], lhsT=svm[:, h, m * 128:(m + 1) * 128], rhs=mo_ap[:, h, :],
                                           start=first, stop=(h == 3))
                            first = False
                    return ins
                rds = [slm_b]
                if mo_ap is not None:
                    rds.append(mo_b)
                if extra_k is not None:
                    rds += [sle_b, extra_k[1]]
                P(mmf, r=rds, w=[pb])
                if pend[0] is not None:
                    pend[0]()
                    pend[0] = None
                V(lambda e, ps=ps, m=m: e.tensor_tensor(out=x_ap[:, m, :], in0=ps, in1=x_ap[:, m, :], op=ALU.add),
                  r=[pb, x_b[m]], w=[x_b[m]], add_w=True)
                if hook:
                    pend[0] = norm_hook(c, x_ap, x_b, m, next_g)
        if pend[0] is not None:
            pend[0]()

    def seal(x_b):
        pass

    c = make_ctx("pm")
    ar = c.ar
    memtm = [ar.alloc([128, 2, D], F32, "memtm") for _ in range(2)]
    memT, memT_b = ar.alloc([128, 8, NMEM], F32, "memT")
    memh, memh_b = ar.alloc([128, 8, NMEM], BF16, "memh")
    for l in range(2):
        wkv = WL("w_mem_kv", l, D)
        sl, sl_b = wslot(c, [(lambda a: a[:, 0:4096].rearrange("p (k n) -> p k n", k=8), kx(wkv, 0, 512))])
        sv = sl[:, 0:4096].rearrange("p (k n) -> p k n", k=8)
        for s in range(NSEQ):
            mt_ap, mt_b = memtm[(l * NSEQ + s) % 2]
            S.dma("pool", mt_ap, mem_in[s * NMEM:(s + 1) * NMEM, :].rearrange("(t p) d -> p t d", p=128), writes=[mt_b])
            for t in range(2):
                for half in range(2):
                    ps, pb = S.get_psum()

                    def tr(e, ps=ps, t=t, half=half, mt_ap=mt_ap):
                        ins = None
                        for j in range(4):
                            cc = half * 4 + j
                            ins = e.transpose(out=ps[:, j * 128:(j + 1) * 128], in_=mt_ap[:, t, cc * 128:(cc + 1) * 128],
                                              identity=ident)
                        return ins
                    P(tr, r=[mt_b, b_const], w=[pb])
                    A(lambda e, ps=ps, t=t, half=half: e.copy(
                        out=memT[:, half * 4:(half + 1) * 4, t * 128:(t + 1) * 128],
                        in_=ps.rearrange("p (a b) -> p a b", a=4)), r=[pb], w=[memT_b], add_w=not (t == 0 and half == 0))
            if stop == 'pm1':
                S.emit()
                return nc
            rms(c, memT, memT_b, C_MEM + l * 8, memh, memh_b, n=NMEM)
            if stop == 'pm2':
                S.emit()
                return nc
            km, km_b = kmT[(s, l)]
            vm, vm_b = Vm[(s, l)]
            for fc in range(2):
                ps, pb = S.get_psum()

                def mmk(e, ps=ps, fc=fc):
                    ins = None
                    for k in range(8):
                        ins = e.matmul(ps[:, :NMEM], lhsT=sv[:, k, fc * 128:(fc + 1) * 128], rhs=memh[:, k, :],
                                       start=(k == 0), stop=(k == 7))
                    return ins
                P(mmk, r=[sl_b, memh_b], w=[pb])
                if stop == 'pm3':
                    S.emit()
                    return nc
                headnorm(c, ps, pb, C_MK + l, km[:, fc, :], km_b, n=NMEM, add_w=(fc > 0))
                if stop == 'pm4':
                    S.emit()
                    return nc
            for mt in range(2):
                ps, pb = S.get_psum()

                def mmv(e, ps=ps, mt=mt):
                    ins = None
                    for k in range(8):
                        ins = e.matmul(ps[:, :256], lhsT=memh[:, k, mt * 128:(mt + 1) * 128], rhs=sv[:, k, 256:512],
                                       start=(k == 0), stop=(k == 7))
                    return ins
                P(mmv, r=[sl_b, memh_b], w=[pb])
                A(lambda e, ps=ps, mt=mt, vm=vm: e.copy(out=vm[:, mt, :], in_=ps[:, :256]), r=[pb], w=[vm_b], add_w=(mt > 0))
    S.barrier()

    if stop == 'pm':
        S.emit()
        return nc
    c = make_ctx("pa", nx=2, nmo=1)
    ar = c.ar
    xin = [ar.alloc([128, D], F32, "xin") for _ in range(4)]
    qn, qn_b = ar.alloc([128, 12, CH], BF16, "qn")
    vt, vt_b = ar.alloc([128, 4, 768], BF16, "vt")
    rtm, rtm_b = ar.alloc([128, 4], F32, "rtm")
    w_in_a = WB["w_in_a"]

    def load_x_chunk(ci):
        for j in range(4):
            ap, b = xin[j]
            S.dma("pool", ap, x_in[ci * CH + j * 128: ci * CH + (j + 1) * 128, :], writes=[b])

    def transposes(ci):
        x_ap, x_b = c.xT[ci % 2]
        for j in range(4):
            xi, xi_b = xin[j]
            for half in range(2):
                ps, pb = S.get_psum()

                def tr(e):
                    ins = None
                    for jj in range(4):
                        cc = half * 4 + jj
                        ins = e.transpose(out=ps[:, jj * 128:(jj + 1) * 128], in_=xi[:, cc * 128:(cc + 1) * 128], identity=ident)
                    return ins
                P(tr, r=[xi_b, b_const], w=[pb])
                if half == 0:
                    A(lambda e: e.copy(out=x_ap[:, 0:4, j * 128:(j + 1) * 128], in_=ps.rearrange("p (a b) -> p a b", a=4)),
                      r=[pb], w=[x_b[0:4]], add_w=(j > 0))
                else:
                    V(lambda e: e.tensor_copy(out=x_ap[:, 4:8, j * 128:(j + 1) * 128], in_=ps.rearrange("p (a b) -> p a b", a=4)),
                      r=[pb], w=[x_b[4:8]], add_w=(j > 0))

    def first_hooks_act(ci):
        x_ap, x_b = c.xT[ci % 2]
        return [norm_hook(c, x_ap, x_b, k, C_FFN1, mul_on_dve=True) for k in range(8)]

    def prep_next(ci):
        if ci + 1 < NCHK:
            transposes(ci + 1)
            if ci + 2 < NCHK:
                load_x_chunk(ci + 2)

    load_x_chunk(0)
    transposes(0)
    if NCHK > 1:
        load_x_chunk(1)
    for th in first_hooks_act(0):
        th()
    rs1 = norm_finish(c)
    for ci in range(NCHK):
        s = ci // (SEQ // CH)
        t0 = ci * CH
        x_ap, x_b = c.xT[ci % 2]
        ffn(c, x_ap, x_b, 0, 1, rs1, C_MIX, mid=(lambda ci=ci: prep_next(ci)))
        S.dma("pool", XA[:, :, t0:t0 + CH].rearrange("c p t -> p c t"), x_ap, reads=[x_b], writes=[b_XA], add_w=True)
        hb, hb_b = c.hb
        rs2 = norm_finish(c)
        pst, pbt = S.get_psum()

        def trr(e):
            ins = None
            for j in range(4):
                ins = e.transpose(out=pst[:, j * 128:(j + 1) * 128], in_=rs2[0][:, j * 128:(j + 1) * 128], identity=ident)
            return ins
        P(trr, r=[rs2[1], b_const], w=[pbt])
        V(lambda e: e.tensor_copy(out=rtm, in_=pst.rearrange("p (j q) -> p j q", j=4)[:, :, 0]), r=[pbt], w=[rtm_b])
        first_q = True
        pend_h = [None]
        for t in range(5):
            sl, sl_b = wslot(c, [(lambda a: a[:, 0:4096].rearrange("p (k n) -> p k n", k=8), kx(w_in_a, t * 512, 512))])
            sv = sl[:, 0:4096].rearrange("p (k n) -> p k n", k=8)
            if t == 3 or t == 4:
                nv = 512 if t == 3 else 256
                for j in range(4):
                    ps, pb = S.get_psum()

                    def mmv(e, ps=ps, j=j, sv=sv, nv=nv):
                        ins = None
                        for k in range(8):
                            ins = e.matmul(ps[:, :nv], lhsT=hb[:, k, j * 128:(j + 1) * 128], rhs=sv[:, k, 0:nv],
                                           start=(k == 0), stop=(k == 7))
                        return ins
                    P(mmv, r=[sl_b, hb_b], w=[pb])
                    if pend_h[0] is not None:
                        pend_h[0]()
                        pend_h[0] = None
                    A(lambda e, ps=ps, j=j, nv=nv, t=t: e.mul(out=vt[:, j, (t - 3) * 512:(t - 3) * 512 + nv], in_=ps[:, :nv],
                                                              mul=rtm[:, j:j + 1]),
                      r=[pb, rtm_b], w=[vt_b], add_w=not (t == 3 and j == 0))
            blocks = {0: [0, 1, 2, 3], 1: [0, 1, 2, 3], 2: [0, 1, 2, 3], 3: [], 4: [2, 3]}[t]
            for bi in blocks:
                gcolidx = t * 4 + bi
                ps, pb = S.get_psum()

                def mmq(e, ps=ps, sv=sv, bi=bi):
                    ins = None
                    for k in range(8):
                        ins = e.matmul(ps[:, :], lhsT=sv[:, k, bi * 128:(bi + 1) * 128], rhs=hb[:, k, :],
                                       start=(k == 0), stop=(k == 7))
                    return ins
                P(mmq, r=[sl_b, hb_b], w=[pb])
                if pend_h[0] is not None:
                    pend_h[0]()
                    pend_h[0] = None
                if gcolidx < 6:
                    pend_h[0] = headnorm(c, ps, pb, C_NAQS, qn[:, gcolidx, :], qn_b, add_w=not first_q, rs=rs2, defer=True)
                    first_q = False
                elif gcolidx < 12:
                    pend_h[0] = headnorm(c, ps, pb, C_NAK, qn[:, gcolidx, :], qn_b, add_w=True, rs=rs2, defer=True)
                else:
                    fcq = gcolidx - 18
                    pend_h[0] = headnorm(c, ps, pb, C_MQS + 0, c.qmn[0][:, fcq, :], c.qmn[1], add_w=(fcq > 0), rs=rs2, defer=True)
        if pend_h[0] is not None:
            pend_h[0]()
            pend_h[0] = None
        S.dma("pool", Q_d[:, :, t0:t0 + CH].rearrange("c p t -> p c t"), qn[:, 0:6, :], reads=[qn_b], writes=[b_Q], add_w=True)
        S.dma("pool", K_d[:, :, t0:t0 + CH].rearrange("c p t -> p c t"), qn[:, 6:12, :], reads=[qn_b], writes=[b_K], add_w=True)
        S.dma("pool", V_d[t0:t0 + CH, :].rearrange("(j p) n -> p j n", p=128), vt, reads=[vt_b], writes=[b_V], add_w=True)
        ths = first_hooks_act(ci + 1) if ci + 1 < NCHK else []
        mo, mo_b = c.mo[ci % len(c.mo)]
        mem_attn(c, s, 0, mo, mo_b)
        S.dma("pool", MO_d[:, :, t0:t0 + CH], mo, reads=[mo_b], writes=[b_MO], add_w=True)
        for th in ths:
            th()
        if ths:
            rs1 = norm_finish(c)
    S.barrier()

    if stop == 'pa':
        S.emit()
        return nc
    ar = Arena(nc, PERM_END, "pb")
    TT, TT_b = ar.alloc([128, 12, 14, 64], F32, "TT")
    Qs = [ar.alloc([128, SEQ], BF16, "Qs") for _ in range(2)]
    Ks = [ar.alloc([128, SEQ], BF16, "Ks") for _ in range(2)]
    Ve = [ar.alloc([128, 64, 128], BF16, "Ve") for _ in range(2)]
    Vo = [ar.alloc([128, 63, 128], BF16, "Vo") for _ in range(2)]
    st = [ar.alloc([128, 256], F32, "st") for _ in range(8)]
    pt = [ar.alloc([128, 8, 64], BF16, "pt") for _ in range(5)]
    recb = [ar.alloc([64, 256], F32, "recb") for _ in range(3)]
    ot = [ar.alloc([64, 1024], BF16, "ot") for _ in range(3)]
    V(lambda e: e.memset(TT, -1e30), w=[TT_b])
    for par in range(2):
        for kc in range(64):
            qlo = 0 if kc <= 15 else kc - 7
            qhi = 63 if kc >= 48 else kc + 8
            nq = qhi - qlo + 1
            c0 = 15 - kc + qlo
            p = par * 64 + kc
            S.dma("sp", TT[p:p + 1, :, :, qlo:qhi + 1], rpb_in[:, par:par + 14, c0:c0 + nq].rearrange("(o h) j q -> o h j q", o=1),
                  writes=[TT_b], add_w=not (par == 0 and kc == 0))
    groups = []
    for s in range(NSEQ):
        for fc in range(6):
            groups.append((s, fc))

    def load_group(gi):
        s, fc = groups[gi]
        Qa, Q_b = Qs[gi % 2]
        Ka, K_b = Ks[gi % 2]
        Vea, Ve_b = Ve[gi % 2]
        Voa, Vo_b = Vo[gi % 2]
        tb = s * SEQ
        S.dma("sp", Qa, Q_d[fc, :, tb:tb + SEQ], reads=[b_Q], writes=[Q_b])
        S.dma("sp", Ka, K_d[fc, :, tb:tb + SEQ], reads=[b_K], writes=[K_b])
        S.dma("sp", Vea, V_d[tb:tb + SEQ, fc * 128:(fc + 1) * 128].rearrange("(i p) n -> p i n", p=128),
              reads=[b_V], writes=[Ve_b])
        S.dma("sp", Voa, V_d[tb + 64:tb + 64 + 63 * 128, fc * 128:(fc + 1) * 128].rearrange("(i p) n -> p i n", p=128),
              reads=[b_V], writes=[Vo_b])

    units = []
    for gi, (s, fc) in enumerate(groups):
        for hp in range(2):
            for rg in range(8):
                for r4 in range(4):
                    for rp in range(2):
                        units.append(dict(gi=gi, s=s, fc=fc, hp=hp, rg=rg, r4=r4, rp=rp,
                                          rows=[rg * 16 + r4 * 4 + rp * 2 + i for i in range(2)],
                                          first=(hp == 0 and rg == 0 and r4 == 0 and rp == 0)))
    st_i = [0]
    pt_i = [0]
    ot_i = [0]
    rc_i = [0]
    cur = {}
    TTv = TT.rearrange("p h (a two) q -> p h a two q", two=2)

    def u_qk(u):
        gi, hp = u["gi"], u["hp"]
        Qa, Q_b = Qs[gi % 2]
        Ka, K_b = Ks[gi % 2]
        pl, ph = hp * 64, hp * 64 + 64
        pss, pbs = S.get_psum()
        u["pss"], u["pbs"] = pss, pbs
        rows = u["rows"]

        def mmqk(e):
            ins = None
            for i, r in enumerate(rows):
                rs = min(max(r - 4, 0), 120)
                for t in range(4):
                    k0 = rs * 64 + t * 128
                    ins = e.matmul(pss[:, (i * 4 + t) * 64:(i * 4 + t + 1) * 64],
                                   lhsT=Ka[pl:ph, k0:k0 + 128], rhs=Qa[pl:ph, r * 64:(r + 1) * 64],
                                   start=True, stop=True)
            return ins
        P(mmqk, r=[K_b, Q_b], w=[pbs])

    def u_sm(u):
        h = u["fc"] * 2 + u["hp"]
        pss, pbs = u["pss"], u["pbs"]
        p_ap, p_b = pt[pt_i[0] % len(pt)]
        pt_i[0] += 1
        u["p_ap"], u["p_b"] = p_ap, p_b
        for i, r in enumerate(u["rows"]):
            rs = min(max(r - 4, 0), 120)
            dr0 = rs - r + 7
            s_ap, s_b = st[st_i[0] % len(st)]
            st_i[0] += 1
            V(lambda e: e.tensor_tensor(
                out=s_ap.rearrange("p (t q) -> p t q", t=4),
                in0=pss[:, i * 256:(i + 1) * 256].rearrange("p (t q) -> p t q", t=4),
                in1=TTv[:, h, dr0 // 2:dr0 // 2 + 4, dr0 % 2, :], op=ALU.add),
              r=[pbs, TT_b], w=[s_b])
            A(lambda e: e.activation(out=p_ap[:, i * 4:(i + 1) * 4, :], in_=s_ap.rearrange("p (t q) -> p t q", t=4),
                                     func=AF.Exp),
              r=[s_b], w=[p_b], add_w=(i > 0))

    def u_pv(u):
        gi, hp, rp, r4, rg, fc, s = u["gi"], u["hp"], u["rp"], u["r4"], u["rg"], u["fc"], u["s"]
        Vea, Ve_b = Ve[gi % 2]
        Voa, Vo_b = Vo[gi % 2]
        pl, ph = hp * 64, hp * 64 + 64
        if rp == 0:
            cur["pso"], cur["pbo"] = S.get_psum()
        pso, pbo = cur["pso"], cur["pbo"]
        p_ap, p_b = u["p_ap"], u["p_b"]
        rows = u["rows"]

        def mmpv(e):
            ins = None
            for i, r in enumerate(rows):
                rs = min(max(r - 4, 0), 120)
                col = (rp * 2 + i) * 64
                for t in range(4):
                    if rs % 2 == 0:
                        vl = Vea[:, rs // 2 + t, pl:ph]
                    else:
                        vl = Voa[:, (rs - 1) // 2 + t, pl:ph]
                    ins = e.matmul(pso[0:64, col:col + 64], lhsT=vl, rhs=p_ap[:, i * 4 + t, :],
                                   start=(t == 0), stop=(t == 3))
                for t in range(4):
                    ins = e.matmul(pso[0:64, 256 + col:256 + col + 64], lhsT=ones_bf[:, 0:64],
                                   rhs=p_ap[:, i * 4 + t, :], start=(t == 0), stop=(t == 3))
            return ins
        P(mmpv, r=[p_b, Ve_b, Vo_b, b_const], w=[pbo], add_w=(rp > 0))
        u["pso"], u["pbo"] = pso, pbo

    def u_norm(u):
        hp, rp, r4, rg, fc, s = u["hp"], u["rp"], u["r4"], u["rg"], u["fc"], u["s"]
        pl, ph = hp * 64, hp * 64 + 64
        pso, pbo = u["pso"], u["pbo"]
        if rp == 1:
            if r4 == 0:
                cur["ot"] = ot[ot_i[0] % len(ot)]
                ot_i[0] += 1
            o_ap, o_b = cur["ot"]
            rc, rc_b = recb[rc_i[0] % len(recb)]
            rc_i[0] += 1
            A(lambda e: e.activation(out=rc, in_=pso[0:64, 256:512], func=AF.Ln), r=[pbo], w=[rc_b])
            A(lambda e: e.activation(out=rc, in_=rc, func=AF.Exp, scale=-1.0), r=[rc_b], w=[rc_b])
            V(lambda e: e.tensor_tensor(out=o_ap[:, r4 * 256:(r4 + 1) * 256], in0=rc, in1=pso[0:64, 0:256], op=ALU.mult),
              r=[rc_b, pbo], w=[o_b], add_w=(r4 > 0))
            if r4 == 3:
                tb = s * SEQ
                S.dma("pool", O_d[fc, pl:ph, tb + rg * 1024: tb + (rg + 1) * 1024], o_ap, reads=[o_b], writes=[b_O], add_w=True)

    DEPTH = 2
    NLAG = 2
    load_group(0)
    for i in range(len(units) + DEPTH + NLAG):
        if i < len(units):
            u_qk(units[i])
        if DEPTH <= i < len(units) + DEPTH:
            u = units[i - DEPTH]
            if u["first"] and u["gi"] + 1 < len(groups):
                load_group(u["gi"] + 1)
            u_sm(u)
            u_pv(u)
        if i >= DEPTH + NLAG:
            u_norm(units[i - DEPTH - NLAG])
    S.barrier()

    if stop == 'pb':
        S.emit()
        return nc
    c = make_ctx("pc")
    ar = c.ar
    oin = [ar.alloc([128, 6, CH], BF16, "oin") for _ in range(2)]
    moin = [ar.alloc([64, 4, CH], BF16, "moin") for _ in range(2)]
    zT, zT_b = ar.alloc([128, 6, CH], BF16, "zT")
    abt = [ar.alloc([128, 2, 768], BF16, "abt") for _ in range(2)]
    w_in_b = WB["w_in_b"]

    def load_c(ci):
        t0 = ci * CH
        x_ap, x_b = c.xT[ci % 2]
        S.dma("pool", x_ap, XA[:, :, t0:t0 + CH].rearrange("c p t -> p c t"), reads=[b_XA], writes=[x_b])
        S.dma("pool", oin[ci % 2][0], O_d[:, :, t0:t0 + CH].rearrange("c p t -> p c t"), reads=[b_O], writes=[oin[ci % 2][1]])
        S.dma("pool", moin[ci % 2][0], MO_d[:, :, t0:t0 + CH], reads=[b_MO], writes=[moin[ci % 2][1]])

    b_XA2 = Buf("XA2")
    load_c(0)
    for ci in range(NCHK):
        s = ci // (SEQ // CH)
        t0 = ci * CH
        x_ap, x_b = c.xT[ci % 2]
        if ci + 1 < NCHK:
            load_c(ci + 1)
        mo_project(c, "w_out_a", moin[ci % 2][0], moin[ci % 2][1], x_ap, x_b, extra_k=oin[ci % 2], hook=True, next_g=C_FFN2)
        rsA = norm_finish(c)
        ffn(c, x_ap, x_b, 0, 2, rsA, None)
        rsB = norm_finish(c)
        for k in range(8):
            V(lambda e: e.scalar_tensor_tensor(out=x_ap[:, k, :], in0=x_ap[:, k, :], scalar=cols[:, C_OUT + k:C_OUT + k + 1],
                                               in1=rsB[0], op0=ALU.mult, op1=ALU.mult),
              r=[x_b[k], rsB[1], b_cols], w=[x_b[k]], add_w=True)
            norm_hook(c, x_ap, x_b, k, C_FFN1 + 8)()
        rsC = norm_finish(c)
        ffn(c, x_ap, x_b, 1, 1, rsC, C_MIX + 8)
        hb, hb_b = c.hb
        rsD = norm_finish(c)
        pend_h = [None]
        for t in range(2):
            sl, sl_b = wslot(c, [(lambda a: a[:, 0:4096].rearrange("p (k n) -> p k n", k=8), kx(w_in_b, t * 512, 512))])
            sv = sl[:, 0:4096].rearrange("p (k n) -> p k n", k=8)
            for bi in range(4):
                g = t * 4 + bi
                ps, pb = S.get_psum()

                def mmz(e, ps=ps, sv=sv, bi=bi):
                    ins = None
                    for k in range(8):
                        ins = e.matmul(ps[:, :], lhsT=sv[:, k, bi * 128:(bi + 1) * 128], rhs=hb[:, k, :],
                                       start=(k == 0), stop=(k == 7))
                    return ins
                P(mmz, r=[sl_b, hb_b], w=[pb])
                if pend_h[0] is not None:
                    pend_h[0]()
                    pend_h[0] = None
                if g < 6:
                    V(lambda e, ps=ps, g=g: e.tensor_tensor(out=zT[:, g, :], in0=ps, in1=rsD[0], op=ALU.mult),
                      r=[pb, rsD[1]], w=[zT_b], add_w=(g > 0))
                else:
                    pend_h[0] = headnorm(c, ps, pb, C_MQS + 1, c.qmn[0][:, g - 6, :], c.qmn[1], add_w=(g > 6), rs=rsD, defer=True)
        if pend_h[0] is not None:
            pend_h[0]()
            pend_h[0] = None
        mo, mo_b = c.mo[ci % 2]
        mem_attn(c, s, 1, mo, mo_b)
        mo_project(c, "w_out_b", mo, mo_b, x_ap, x_b, extra_k=None)
        S.dma("pool", XA[:, :, t0:t0 + CH].rearrange("c p t -> p c t"), x_ap, reads=[x_b], writes=[b_XA2], add_w=True)
        dsl = []
        for src_in in (cd_in, sd_in):
            ap_, b_ = wslot(c, [(lambda a: a[:, 0:4608].rearrange("p (k n) -> p k n", k=6),
                                 src_in.rearrange("(k p) n -> p k n", p=128))])
            dsl.append((ap_[:, 0:4608].rearrange("p (k n) -> p k n", k=6), b_))
        for j in range(4):
            ab, ab_b = abt[j % 2]
            for wi, (Ws, W_b) in enumerate(dsl):
                for half in range(2):
                    ps, pb = S.get_psum()

                    def mma(e, ps=ps, j=j, Ws=Ws, half=half):
                        ins = None
                        for k in range(6):
                            ins = e.matmul(ps[:, :384], lhsT=zT[:, k, j * 128:(j + 1) * 128],
                                           rhs=Ws[:, k, half * 384:(half + 1) * 384], start=(k == 0), stop=(k == 5))
                        return ins
                    P(mma, r=[zT_b, W_b], w=[pb])
                    if half == 0:
                        A(lambda e, ps=ps, ab=ab, wi=wi, half=half: e.copy(out=ab[:, wi, half * 384:(half + 1) * 384], in_=ps[:, :384]),
                          r=[pb], w=[ab_b], add_w=not (wi == 0 and half == 0))
                    else:
                        V(lambda e, ps=ps, ab=ab, wi=wi, half=half: e.tensor_copy(out=ab[:, wi, half * 384:(half + 1) * 384], in_=ps[:, :384]),
                          r=[pb], w=[ab_b], add_w=True)
            r0 = (t0 - s * SEQ + j * 128) // 64
            S.dma("pool", A_d[s, :, r0:r0 + 2, :, :].rearrange("n a b m -> (a b) n m"),
                  ab[:, 0, :].rearrange("p (n m) -> p n m", n=6), reads=[ab_b], writes=[b_A], add_w=True)
            S.dma("pool", B_d[s, :, r0:r0 + 2, :, :].rearrange("n a b m -> (a b) n m"),
                  ab[:, 1, :].rearrange("p (n m) -> p n m", n=6), reads=[ab_b], writes=[b_B], add_w=True)
    S.barrier()

    if stop == 'pc':
        S.emit()
        return nc
    ar = Arena(nc, PERM_END, "pd")
    c128, c128_b = ar.alloc([128, 3, 128], BF16, "c128")
    Et, E_b = ar.alloc([64, 2, SEQ], BF16, "E")
    ATs = [ar.alloc([128, 64, 128], BF16, "AT") for _ in range(2)]
    BTs = [ar.alloc([128, 64, 128], BF16, "BT") for _ in range(2)]
    Yc, Yc_b = ar.alloc([64, 2, 64, 128], BF16, "Yc")
    FTt = [ar.alloc([64, SEQ], BF16, "FTt") for _ in range(2)]
    S.dma("sp", c128, c128_in, writes=[c128_b])
    S.dma("sp", Et, e_in, writes=[E_b])
    kk = 0
    for s in range(NSEQ):
        for nb in range(6):
            AT, AT_b = ATs[kk % 2]
            BT, BT_b = BTs[kk % 2]
            kk += 1
            S.dma("sp", AT, A_d[s, nb], reads=[b_A], writes=[AT_b])
            S.dma("sp", BT, B_d[s, nb], reads=[b_B], writes=[BT_b])
            for mh in range(2):
                for m4 in range(16):
                    psr, pbr = S.get_psum()
                    psi, pbi = S.get_psum()

                    def mm1(e):
                        ins = None
                        for mm_ in range(4):
                            m = mh * 64 + m4 * 4 + mm_
                            o = mm_ * 128
                            e.matmul(psr[0:64, o:o + 128], lhsT=AT[:, :, m], rhs=c128[:, 0, :], start=True, stop=False)
                            e.matmul(psr[0:64, o:o + 128], lhsT=BT[:, :, m], rhs=c128[:, 1, :], start=False, stop=True)
                            e.matmul(psi[0:64, o:o + 128], lhsT=AT[:, :, m], rhs=c128[:, 1, :], start=True, stop=False)
                            ins = e.matmul(psi[0:64, o:o + 128], lhsT=BT[:, :, m], rhs=c128[:, 2, :], start=False, stop=True)
                        return ins
                    P(mm1, r=[AT_b, BT_b, c128_b], w=[pbr, pbi])
                    A(lambda e: e.copy(out=Yc[:, 0, m4 * 4:(m4 + 1) * 4, :], in_=psr[0:64, :].rearrange("p (m k) -> p m k", m=4)),
                      r=[pbr], w=[Yc_b], add_w=(m4 > 0))
                    V(lambda e: e.tensor_copy(out=Yc[:, 1, m4 * 4:(m4 + 1) * 4, :], in_=psi[0:64, :].rearrange("p (m k) -> p m k", m=4)),
                      r=[pbi], w=[Yc_b], add_w=True)
                F_ap, F_b = FTt[(kk * 2 + mh) % 2]
                Ev = Et.rearrange("p r (k2 k1) -> p r k1 k2", k1=128)
                for kg in range(16):
                    ps, pb = S.get_psum()

                    def mm3(e):
                        ins = None
                        for kl in range(8):
                            k1 = kg * 8 + kl
                            e.matmul(ps[0:64, kl * 64:(kl + 1) * 64], lhsT=Yc[:, 0, :, k1], rhs=Ev[:, 0, k1, :], start=True, stop=False)
                            ins = e.matmul(ps[0:64, kl * 64:(kl + 1) * 64], lhsT=Yc[:, 1, :, k1], rhs=Ev[:, 1, k1, :], start=False, stop=True)
                        return ins
                    P(mm3, r=[Yc_b, E_b], w=[pb])
                    Fo = F_ap.rearrange("p (k2 k1) -> p k1 k2", k1=128)[:, kg * 8:(kg + 1) * 8, :]
                    if kg % 2 == 0:
                        V(lambda e: e.tensor_copy(out=Fo, in_=ps[0:64, :].rearrange("p (a b) -> p a b", a=8)),
                          r=[pb], w=[F_b], add_w=(kg > 0))
                    else:
                        A(lambda e: e.copy(out=Fo, in_=ps[0:64, :].rearrange("p (a b) -> p a b", a=8)),
                          r=[pb], w=[F_b], add_w=True)
                S.dma("pool", FT_d[nb, mh * 64:(mh + 1) * 64, s * SEQ:(s + 1) * SEQ], F_ap, reads=[F_b], writes=[b_FT], add_w=True)
    S.barrier()

    if stop == 'pd':
        S.emit()
        return nc
    c = make_ctx("pe")
    ar = c.ar
    fin = [ar.alloc([128, 6, CH], BF16, "fin") for _ in range(2)]
    yt = [ar.alloc([128, D], F32, "yt") for _ in range(2)]
    grow, grow_b = ar.alloc([128, D], F32, "grow")
    rtmE, rtmE_b = ar.alloc([128, 4], F32, "rtmE")
    S.dma("sp", grow, grow_in, writes=[grow_b])

    def load_e(ci):
        t0 = ci * CH
        x_ap, x_b = c.xT[ci % 2]
        S.dma("pool", x_ap, XA[:, :, t0:t0 + CH].rearrange("c p t -> p c t"), reads=[b_XA2], writes=[x_b])
        S.dma("pool", fin[ci % 2][0], FT_d[:, :, t0:t0 + CH].rearrange("c p t -> p c t"), reads=[b_FT], writes=[fin[ci % 2][1]])

    load_e(0)
    for ci in range(NCHK):
        t0 = ci * CH
        x_ap, x_b = c.xT[ci % 2]
        if ci + 1 < NCHK:
            load_e(ci + 1)
        mo_project(c, "w_out_b", None, None, x_ap, x_b, extra_k=fin[ci % 2], hook=True, next_g=C_FFN2 + 8)
        rsE = norm_finish(c)
        ffn(c, x_ap, x_b, 1, 2, rsE, None)
        rsF = norm_finish(c)
        pst, pbt = S.get_psum()

        def trr(e):
            ins = None
            for j in range(4):
                ins = e.transpose(out=pst[:, j * 128:(j + 1) * 128], in_=rsF[0][:, j * 128:(j + 1) * 128], identity=ident)
            return ins
        for j in range(4):
            y_ap, y_b = yt[j % 2]
            pss_ = []
            for half in range(2):
                ps, pb = S.get_psum()

                def trb(e):
                    ins = None
                    for jj in range(4):
                        cc = half * 4 + jj
                        ins = e.transpose(out=ps[:, jj * 128:(jj + 1) * 128], in_=x_ap[:, cc, j * 128:(j + 1) * 128], identity=ident)
                    return ins
                P(trb, r=[x_b, b_const], w=[pb])
                pss_.append((ps, pb))
            if j == 0:
                P(trr, r=[rsF[1], b_const], w=[pbt])
                V(lambda e: e.tensor_copy(out=rtmE, in_=pst.rearrange("p (j q) -> p j q", j=4)[:, :, 0]), r=[pbt], w=[rtmE_b])
            for half in range(2):
                ps, pb = pss_[half]
                V(lambda e: e.scalar_tensor_tensor(out=y_ap[:, half * 512:(half + 1) * 512], in0=ps, scalar=rtmE[:, j:j + 1],
                                                   in1=grow[:, half * 512:(half + 1) * 512], op0=ALU.mult, op1=ALU.mult),
                  r=[pb, rtmE_b, grow_b], w=[y_b], add_w=(half > 0))
            S.dma("pool", y_out[t0 + j * 128:t0 + (j + 1) * 128, :], y_ap, reads=[y_b])

    S.emit()
    return nc


def _consts():
    bf = ml_dtypes.bfloat16
    n = np.arange(768)
    ang = 2 * np.pi * np.outer(n, n) / 768.0
    cd = (np.cos(ang) / np.sqrt(768.0)).astype(bf)
    sd = (np.sin(ang) / np.sqrt(768.0)).astype(bf)
    t = np.arange(128)
    a = 2 * np.pi * np.outer(t, t) / 128.0
    c128 = np.stack([np.cos(a), -np.sin(a), -np.cos(a)], axis=1) / np.sqrt(128.0)
    t2 = np.arange(64)
    k = np.arange(SEQ)
    ae = 2 * np.pi * np.outer(t2, k) / float(SEQ)
    e = np.stack([np.cos(ae), np.sin(ae)], axis=1) / np.sqrt(64.0)
    return cd, sd, c128.astype(bf), e.astype(bf)


_CACHE = {}


def _prep_common(inputs):
    f = lambda a: np.ascontiguousarray(np.asarray(a, dtype=np.float32))
    cols = np.zeros((128, NCOL), np.float32)
    for base, name in ((C_FFN1, "norm_ffn1"), (C_MIX, "norm_mix"), (C_MEM, "norm_mem"),
                       (C_FFN2, "norm_ffn2"), (C_OUT, "norm_out")):
        g = f(inputs[name])
        cols[:, base:base + 16] = g.reshape(2, 8, 128).transpose(2, 0, 1).reshape(128, 16)
    for base, name, nl in ((C_MQ, "mem_q_norm", 2), (C_MK, "mem_k_norm", 2), (C_NAQ, "na_q_norm", 1), (C_NAK, "na_k_norm", 1)):
        g = f(inputs[name])
        cols[:, base:base + nl] = np.tile(g.T, (2, 1))
    cd, sd, c128, e = _consts()
    common = {
        "cols_in": cols,
        "rpb_rev": np.ascontiguousarray(f(inputs["na_rpb"])[0][:, :, ::-1]),
        "grow_in": np.ascontiguousarray(np.broadcast_to(f(inputs["norm_out"])[1][None, :], (128, D))),
        "cd_in": cd, "sd_in": sd, "c128_in": c128, "e_in": e,
    }
    for (n, r, c) in WSPEC:
        common[n] = f(inputs[n]).reshape(r, c)
    return common


def kernel(**inputs):
    NSEQ = 2
    if "nc" not in _CACHE:
        _CACHE["nc"] = build_program(NSEQ)
    nc = _CACHE["nc"]
    common = _prep_common(inputs)
    xp = np.asarray(inputs["x_prompt"], dtype=np.float32)
    xs = np.asarray(inputs["x_sample"], dtype=np.float32)
    mp = np.asarray(inputs["mem_prompt"], dtype=np.float32)
    ms = np.asarray(inputs["mem_sample"], dtype=np.float32)
    in_maps = []
    for c in range(8):
        d = dict(common)
        d["x_in"] = np.ascontiguousarray(np.concatenate([xs[c], xp[c % 4]], axis=0))
        d["mem_in"] = np.ascontiguousarray(np.concatenate([ms[c], mp[c % 4]], axis=0))
        in_maps.append(d)
    res = run_bass_kernel_spmd(nc, in_maps, core_ids=list(range(8)))
    y_s = np.stack([res.results[c]["y_out"][:SEQ] for c in range(8)], axis=0)
    y_p = np.stack([res.results[c]["y_out"][SEQ:] for c in range(4)], axis=0)
    return (y_p.astype(np.float32), y_s.astype(np.float32))
```

```python
import contextlib
import types
import numpy as np
import ml_dtypes
import concourse.bass as bass
import concourse.mybir as mybir
from concourse.bass_utils import run_bass_kernel_spmd

F32 = mybir.dt.float32
BF16 = mybir.dt.bfloat16
AF = mybir.ActivationFunctionType
ALU = mybir.AluOpType

D = 1024
SEQ = 8192
DFF = 2816
NMEM = 256
CH = 512
SB_BASE = 17408
SB_LIMIT = 229312


class Buf:
    __slots__ = ("name", "w", "r", "pw", "pr", "excl")

    def __init__(self, name="", excl=False):
        self.name = name
        self.excl = excl
        self.w = []
        self.r = []
        self.pw = []
        self.pr = []


class Sched:
    ENGS = ("pe", "act", "dve", "pool", "sp")

    def __init__(self, nc, n_dma_sems=10):
        self.nc = nc
        self.ops = {e: [] for e in self.ENGS}
        self.cnt = {e: 0 for e in self.ENGS}
        self.n_dma_sems = n_dma_sems
        self.dma_i = {e: 0 for e in ("sp", "act", "pool")}
        self.dma_last = {e: [None] * n_dma_sems for e in ("sp", "act", "pool")}
        self.bar = []
        self.bar_id = 0
        self.passed = {e: 0 for e in self.ENGS}
        self.psum = []
        self.ps_i = 0

    def _deps(self, eng, issue_eng, reads, writes, extra, add_w):
        waits = list(extra)
        if self.passed[issue_eng] != self.bar_id:
            waits.extend(self.bar)
            self.passed[issue_eng] = self.bar_id
        for b in reads:
            waits.extend(b.w)
            if b.excl:
                waits.extend(b.r)
        for b in writes:
            if add_w:
                waits.extend(b.pw)
                waits.extend(b.pr)
            else:
                waits.extend(b.w)
            waits.extend(b.r)
        return waits

    @staticmethod
    def _compact(lst):
        if len(lst) <= 16:
            return lst
        d = {}
        for t in lst:
            if t[0] not in d or d[t[0]] < t[1]:
                d[t[0]] = t[1]
        return list(d.items())

    def _commit(self, tok, reads, writes, add_w):
        for b in reads:
            b.r = self._compact(b.r + [tok])
        for b in writes:
            if add_w:
                b.w = self._compact(b.w + [tok])
            else:
                b.pw = b.w
                b.pr = b.r
                b.w = [tok]
                b.r = []

    @staticmethod
    def _freeze(fn):
        if fn.__closure__ is None:
            return fn
        cells = []
        for c_ in fn.__closure__:
            try:
                cells.append(types.CellType(c_.cell_contents))
            except ValueError:
                cells.append(c_)
        g = types.FunctionType(fn.__code__, fn.__globals__, fn.__name__, fn.__defaults__, tuple(cells))
        g.__kwdefaults__ = fn.__kwdefaults__
        return g

    @staticmethod
    def _flat(lst):
        out = []
        for b in lst:
            if isinstance(b, (list, tuple)):
                out.extend(Sched._flat(b))
            else:
                out.append(b)
        return out

    def op(self, eng, fn, reads=(), writes=(), extra=(), add_w=False):
        fn = self._freeze(fn)
        reads, writes = self._flat(reads), self._flat(writes)
        waits = self._deps(eng, eng, reads, writes, extra, add_w)
        self.cnt[eng] += 1
        tok = (eng, self.cnt[eng])
        self.ops[eng].append((waits, fn, ("self", 1)))
        self._commit(tok, reads, writes, add_w)
        return tok

    def dma(self, q, out_ap, in_ap, reads=(), writes=(), extra=(), add_w=False, **kw):
        reads, writes = self._flat(reads), self._flat(writes)
        waits = self._deps("dma_" + q, q, reads, writes, extra, add_w)
        i = self.dma_i[q]
        self.dma_i[q] += 1
        slot = i % self.n_dma_sems
        val = 16 * (i // self.n_dma_sems + 1)
        key = "dma_%s_%d" % (q, slot)
        prev = self.dma_last[q][slot]
        if prev is not None:
            waits.append(prev)
        tok = (key, val)
        self.dma_last[q][slot] = tok

        def fn(e, out_ap=out_ap, in_ap=in_ap, kw=kw):
            return e.dma_start(out=out_ap, in_=in_ap, **kw)
        self.ops[q].append((waits, fn, (key, 16)))
        self._commit(tok, reads, writes, add_w)
        return tok

    def all_tokens(self):
        toks = []
        for q in ("sp", "act", "pool"):
            for t in self.dma_last[q]:
                if t is not None:
                    toks.append(t)
        for e in self.ENGS:
            if self.cnt[e] > 0:
                toks.append((e, self.cnt[e]))
        return toks

    def barrier(self):
        self.bar = self.all_tokens()
        self.bar_id += 1

    def get_psum(self):
        ap, b = self.psum[self.ps_i % len(self.psum)]
        self.ps_i += 1
        return ap, b

    def emit(self):
        nc = self.nc
        keys = list(self.ENGS)
        for q in ("sp", "act", "pool"):
            for s in range(self.n_dma_sems):
                keys.append("dma_%s_%d" % (q, s))
        with contextlib.ExitStack() as st:
            sems = {k: st.enter_context(nc.semaphore("s_" + k)) for k in keys}
            block = st.enter_context(nc.Block())
            final_waits = self.all_tokens()

            def run(engname, engine):
                known = {}
                for waits, fn, sig in self.ops[engname]:
                    for (k, v) in waits:
                        if known.get(k, 0) < v:
                            if k == engname and v > known.get("__self_emitted", 0):
                                raise RuntimeError("self-deadlock on %s" % engname)
                            engine.wait_ge(sems[k], v)
                            known[k] = v
                    ins = fn(engine)
                    if sig[0] == "self":
                        ins.then_inc(sems[engname], 1)
                        known["__self_emitted"] = known.get("__self_emitted", 0) + 1
                    else:
                        ins.then_inc(sems[sig[0]], sig[1])
                if engname == "sp":
                    for (k, v) in final_waits:
                        if known.get(k, 0) < v:
                            engine.wait_ge(sems[k], v)
                            known[k] = v

            @block.tensor
            def _(e):
                run("pe", e)

            @block.scalar
            def _(e):
                run("act", e)

            @block.vector
            def _(e):
                run("dve", e)

            @block.gpsimd
            def _(e):
                run("pool", e)

            @block.sync
            def _(e):
                run("sp", e)


class Arena:
    def __init__(self, nc, base, tag):
        self.nc, self.off, self.tag, self.n = nc, base, tag, 0

    def alloc(self, shape, dtype, name=None):
        esz = 4 if dtype == F32 else 2
        nbytes = esz
        for s in shape[1:]:
            nbytes *= s
        nbytes = (nbytes + 63) // 64 * 64
        self.n += 1
        nm = "%s_%s_%d" % (self.tag, name or "t", self.n)
        t = self.nc.alloc_sbuf_tensor_at(nm, list(shape), dtype, offset=self.off)
        self.off += nbytes
        assert self.off <= SB_LIMIT, ("SBUF overflow", self.tag, self.off)
        return t.ap(), Buf(nm)


WSPEC = [
    ("w_ffn1_in", 2 * D, 2 * DFF), ("w_ffn1_out", 2 * DFF, D),
    ("w_ffn2_in", 2 * D, 2 * DFF), ("w_ffn2_out", 2 * DFF, D),
    ("w_mem_kv", 2 * D, 512), ("w_in_a", D, 2560), ("w_out_a", D, D),
    ("w_in_b", D, D), ("w_out_b", D, D),
]
C_FFN1, C_MIX, C_MEM, C_FFN2, C_OUT = 0, 16, 32, 48, 64
C_MQ, C_MK, C_NAQ, C_NAK = 80, 82, 84, 85
NCOL = 86
SLOT_EL = 5632
NSLOT = 4


def build_program(NSEQ, dbg=False, stop=None):
    nc = bass.Bass("TRN2", target_bir_lowering=False)
    T = NSEQ * SEQ
    NCHK = T // CH
    S = Sched(nc)

    def din(name, shape, dt=F32):
        return nc.dram_tensor(name, list(shape), dt, kind="ExternalInput").ap()

    def dscr(name, shape, dt):
        if dbg and not name.startswith("wb_"):
            return nc.dram_tensor(name, list(shape), dt, kind="ExternalOutput").ap()
        return nc.dram_tensor(name, list(shape), dt).ap()

    x_in = din("x_in", [T, D])
    mem_in = din("mem_in", [NSEQ * NMEM, D])
    cols_in = din("cols_in", [128, NCOL])
    rpb_in = din("rpb_rev", [12, 15, 31])
    grow_in = din("grow_in", [128, D])
    cd_in = din("cd_in", [768, 768], BF16)
    sd_in = din("sd_in", [768, 768], BF16)
    c128_in = din("c128_in", [128, 3, 128], BF16)
    e_in = din("e_in", [64, 2, SEQ], BF16)
    W32 = {n: din(n, [r, c]) for (n, r, c) in WSPEC}
    WB = {n: dscr("wb_" + n, [r, c], BF16) for (n, r, c) in WSPEC}
    y_out = nc.dram_tensor("y_out", [T, D], F32, kind="ExternalOutput").ap()

    XA = dscr("XA", [8, 128, T], F32)
    Q_d = dscr("Q_d", [6, 128, T], BF16)
    K_d = dscr("K_d", [6, 128, T], BF16)
    V_d = dscr("V_d", [T, 768], BF16)
    O_d = dscr("O_d", [6, 128, T], BF16)
    MO_d = dscr("MO_d", [64, 4, T], BF16)
    A_d = dscr("A_d", [NSEQ, 6, 128, 64, 128], BF16)
    B_d = dscr("B_d", [NSEQ, 6, 128, 64, 128], BF16)
    FT_d = dscr("FT_d", [6, 128, T], BF16)
    b_WB, b_XA, b_Q, b_K, b_V, b_O, b_MO, b_A, b_B, b_FT = [Buf(n) for n in
        "WB XA Q K V O MO A B FT".split()]

    for i in range(7):
        S.psum.append((nc.alloc_psum_tensor("ps%d" % i, [128, 512], F32).ap(), Buf("ps%d" % i, excl=True)))
    ssq = nc.alloc_psum_tensor("ps_ssq", [128, 512], F32).ap()
    ssq_b = Buf("ps_ssq", excl=True)

    def P(fn, r=(), w=(), **k):
        return S.op("pe", fn, reads=r, writes=w, **k)

    def A(fn, r=(), w=(), **k):
        return S.op("act", fn, reads=r, writes=w, **k)

    def V(fn, r=(), w=(), **k):
        return S.op("dve", fn, reads=r, writes=w, **k)

    def G(fn, r=(), w=(), **k):
        return S.op("pool", fn, reads=r, writes=w, **k)

    perm = Arena(nc, SB_BASE, "perm")
    cols, b_cols = perm.alloc([128, NCOL + 4], F32, "cols")
    ident, b_const = perm.alloc([128, 128], F32, "ident")
    ones_bf, _ = perm.alloc([128, 128], BF16, "ones")
    blk_bf, _ = perm.alloc([128, 128], BF16, "blk")
    epsc, _ = perm.alloc([128, 1], F32, "eps")
    kmT = {}
    Vm = {}
    for s in range(NSEQ):
        for l in range(2):
            kmT[(s, l)] = perm.alloc([128, 2, 256], BF16, "kmT")
            Vm[(s, l)] = perm.alloc([128, 2, 256], BF16, "Vm")
    PERM_END = perm.off

    S.dma("sp", cols[:, 0:NCOL], cols_in, writes=[b_cols])
    G(lambda e: e.memset(ident, 0.0), w=[b_const])
    G(lambda e: e.affine_select(out=ident, in_=ident, pattern=[[-1, 128]], compare_op=ALU.not_equal,
                                fill=1.0, base=0, channel_multiplier=1), r=[b_const], w=[b_const])
    G(lambda e: e.memset(ones_bf, 1.0), w=[b_const])
    G(lambda e: e.memset(blk_bf, 0.0), w=[b_const])
    G(lambda e: e.memset(blk_bf[0:64, 0:64], 1.0), w=[b_const])
    G(lambda e: e.memset(blk_bf[64:128, 64:128], 1.0), w=[b_const])
    G(lambda e: e.memset(epsc, 1e-6), w=[b_const])
    V(lambda e: e.tensor_scalar_mul(out=cols[:, NCOL:NCOL + 2], in0=cols[:, C_MQ:C_MQ + 2], scalar1=0.125),
      r=[b_cols], w=[b_cols])
    V(lambda e: e.tensor_scalar_mul(out=cols[:, NCOL + 2:NCOL + 3], in0=cols[:, C_NAQ:C_NAQ + 1], scalar1=0.125),
      r=[b_cols], w=[b_cols])
    C_MQS, C_NAQS = NCOL, NCOL + 2

    if stop == 'pconst':
        S.emit()
        return nc
    ar = Arena(nc, PERM_END, "p0")
    CW = 2048
    st32 = [ar.alloc([128, CW], F32, "st32") for _ in range(3)]
    st16 = [ar.alloc([128, CW], BF16, "st16") for _ in range(3)]
    it = 0
    for (n, r, c) in WSPEC:
        src = W32[n].rearrange("(p r) c -> p (r c)", p=128)
        dst = WB[n].rearrange("(p r) c -> p (r c)", p=128)
        tot = r * c // 128
        for o in range(0, tot, CW):
            wdt = min(CW, tot - o)
            a32, b32 = st32[it % 3]
            a16, b16 = st16[it % 3]
            S.dma("sp", a32[:, :wdt], src[:, o:o + wdt], writes=[b32])
            import os
            engs_ = os.environ.get("P0_ENG", "dve,act").split(",")
            eng = engs_[it % len(engs_)]
            if eng == "act":
                A(lambda e, a16=a16, a32=a32, wdt=wdt: e.copy(out=a16[:, :wdt], in_=a32[:, :wdt]), r=[b32], w=[b16])
            else:
                S.op(eng, lambda e, a16=a16, a32=a32, wdt=wdt: e.tensor_copy(out=a16[:, :wdt], in_=a32[:, :wdt]),
                     reads=[b32], writes=[b16])
            S.dma(os.environ.get("P0_STQ", "pool"), dst[:, o:o + wdt], a16[:, :wdt], reads=[b16], writes=[b_WB], add_w=True)
            it += 1
    S.barrier()

    def WL(name, l, rows_per_layer):
        return WB[name][l * rows_per_layer:(l + 1) * rows_per_layer, :]

    if stop == 'p0':
        S.emit()
        return nc
    class Ctx:
        pass

    def make_ctx(tag, nx=2, nmo=2):
        c = Ctx()
        ar = Arena(nc, PERM_END, tag)
        c.ar = ar
        c.slots = [ar.alloc([128, SLOT_EL], BF16, "slot") for _ in range(NSLOT)]
        c.slot_i = 0
        c.xT = []
        for _ in range(nx):
            xa_, _xb = ar.alloc([128, 8, CH], F32, "xT")
            c.xT.append((xa_, [Buf("xT%d" % i) for i in range(8)]))
        c.hb = ar.alloc([128, 8, CH], BF16, "hb")
        c.sq = ar.alloc([128, 8, CH], BF16, "sq")
        c.rt = [ar.alloc([128, CH], F32, "rt") for _ in range(2)]
        c.rstd = [ar.alloc([128, CH], F32, "rstd") for _ in range(2)]
        c.act = ar.alloc([128, 22, CH], BF16, "act")
        c.sg = [ar.alloc([128, CH], F32, "sg") for _ in range(2)]
        c.qf = [ar.alloc([128, CH], F32, "qf") for _ in range(3)]
        c.sqh = [ar.alloc([128, CH], BF16, "sqh") for _ in range(3)]
        c.pT = [ar.alloc([128, CH], BF16, "pT") for _ in range(4)]
        c.mo = [ar.alloc([64, 4, CH], BF16, "mo") for _ in range(nmo)]
        c.qmn = ar.alloc([128, 2, CH], BF16, "qmn")
        c.rec = [ar.alloc([64, CH], F32, "rec") for _ in range(2)]
        c.hrt = [ar.alloc([128, CH], F32, "hrt") for _ in range(2)]
        c.hrs = [ar.alloc([128, CH], F32, "hrs") for _ in range(2)]
        c.ru = [ar.alloc([128, CH], F32, "ru") for _ in range(2)]
        c.sqk = [Buf("sqk%d" % i) for i in range(8)]
        c.rs_i = 0
        c.k = 0
        return c

    def wslot(c, parts):
        ap, b = c.slots[c.slot_i % NSLOT]
        c.slot_i += 1
        first = True
        for dst_fn, src in parts:
            S.dma("sp", dst_fn(ap), src, reads=[b_WB], writes=[b], add_w=not first)
            first = False
        return ap, b

    def kx(w, c0, n, kc=8):
        return w[:, c0:c0 + n].rearrange("(k p) n -> p k n", p=128)

    def rr(c, lst):
        c.k += 1
        return lst[c.k % len(lst)]

    def rms(c, x_ap, x_b, gbase, out_ap, out_b, n=CH, nchunk=8, dim=D):
        sq, sq_b = c.sq
        A(lambda e: e.activation(out=sq[:, :nchunk, :n], in_=x_ap, func=AF.Square), r=[x_b], w=[sq_b])
        ps, pb = S.get_psum()

        def mm(e):
            ins = None
            for k in range(nchunk):
                ins = e.matmul(ps[:, :n], lhsT=ones_bf, rhs=sq[:, k, :n], start=(k == 0), stop=(k == nchunk - 1))
            return ins
        P(mm, r=[sq_b, b_const], w=[pb])
        rt, rt_b = rr(c, c.rt)
        rs, rs_b = rr(c, c.rstd)
        A(lambda e: e.activation(out=rt[:, :n], in_=ps[:, :n], func=AF.Sqrt, bias=epsc[:, 0:1], scale=1.0 / dim),
          r=[pb, b_const], w=[rt_b])
        V(lambda e: e.reciprocal(out=rs[:, :n], in_=rt[:, :n]), r=[rt_b], w=[rs_b])
        for k in range(nchunk):
            V(lambda e, k=k: e.scalar_tensor_tensor(out=out_ap[:, k, :], in0=x_ap[:, k, :],
                                                    scalar=cols[:, gbase + k:gbase + k + 1], in1=rs[:, :n],
                                                    op0=ALU.mult, op1=ALU.mult),
              r=[x_b, rs_b, b_cols], w=[out_b], add_w=(k > 0))

    def norm_hook(c, x_ap, x_b, k, gbase, n=CH, mul_on_dve=False):
        hb, hb_b = c.hb
        sq = c.sq[0]
        sqb = c.sqk[k]
        if gbase is not None and mul_on_dve:
            V(lambda e: e.tensor_scalar_mul(out=hb[:, k, :n], in0=x_ap[:, k, :n], scalar1=cols[:, gbase + k:gbase + k + 1]),
              r=[x_b[k], b_cols], w=[hb_b], add_w=(k > 0))
        elif gbase is not None:
            A(lambda e: e.mul(out=hb[:, k, :n], in_=x_ap[:, k, :n], mul=cols[:, gbase + k:gbase + k + 1]),
              r=[x_b[k], b_cols], w=[hb_b], add_w=(k > 0))
        A(lambda e: e.activation(out=sq[:, k, :n], in_=x_ap[:, k, :n], func=AF.Square), r=[x_b[k]], w=[sqb])

        def pe_part():
            P(lambda e: e.matmul(ssq[:, :n], lhsT=ones_bf, rhs=sq[:, k, :n], start=(k == 0), stop=(k == 7)),
              r=[sqb, b_const], w=[ssq_b], add_w=(k > 0))
        return pe_part

    def norm_finish(c, dim=D, n=CH):
        c.rs_i += 1
        rt, rt_b = c.rt[c.rs_i % 2]
        rs, rs_b = c.rstd[c.rs_i % 2]
        A(lambda e: e.activation(out=rt[:, :n], in_=ssq[:, :n], func=AF.Ln, bias=epsc[:, 0:1], scale=1.0 / dim),
          r=[ssq_b, b_const], w=[rt_b])
        A(lambda e: e.activation(out=rs[:, :n], in_=rt[:, :n], func=AF.Exp, scale=-0.5), r=[rt_b], w=[rs_b])
        return rs, rs_b

    def headnorm(c, ps, pb, gcol, out_ap, out_b, n=CH, add_w=False, rs=None, defer=False):
        c.h_i = getattr(c, "h_i", 0) + 1
        sqh, sqh_b = c.sqh[c.h_i % 3]
        qf, qf_b = c.qf[c.h_i % 3]
        if rs is None:
            V(lambda e: e.tensor_copy(out=qf[:, :n], in_=ps[:, :n]), r=[pb], w=[qf_b])
        else:
            V(lambda e: e.tensor_tensor(out=qf[:, :n], in0=ps[:, :n], in1=rs[0][:, :n], op=ALU.mult), r=[pb, rs[1]], w=[qf_b])
        A(lambda e: e.activation(out=sqh[:, :n], in_=qf[:, :n], func=AF.Square), r=[qf_b], w=[sqh_b])

        def part2():
            ps2, pb2 = S.get_psum()
            P(lambda e: e.matmul(ps2[:, :n], lhsT=blk_bf, rhs=sqh[:, :n], start=True, stop=True),
              r=[sqh_b, b_const], w=[pb2])
            c.h2_i = getattr(c, "h2_i", 0) + 1
            rt, rt_b = c.hrt[c.h2_i % 2]
            hs, hs_b = c.hrs[c.h2_i % 2]
            A(lambda e: e.activation(out=rt[:, :n], in_=ps2[:, :n], func=AF.Ln, bias=epsc[:, 0:1], scale=1.0 / 64),
              r=[pb2, b_const], w=[rt_b])
            A(lambda e: e.activation(out=hs[:, :n], in_=rt[:, :n], func=AF.Exp, scale=-0.5), r=[rt_b], w=[hs_b])
            V(lambda e: e.scalar_tensor_tensor(out=out_ap, in0=qf[:, :n], scalar=cols[:, gcol:gcol + 1], in1=hs[:, :n],
                                               op0=ALU.mult, op1=ALU.mult),
              r=[qf_b, hs_b, b_cols], w=[out_b], add_w=add_w)
        if defer:
            return part2
        part2()
        return None

    def ffn(c, x_ap, x_b, l, which, rs, next_g, want_next=True, mid=None):
        hb, hb_b = c.hb
        act, act_b = c.act
        rs_ap, rs_b = rs
        w_in = WL("w_ffn%d_in" % which, l, D)
        w_out = WL("w_ffn%d_out" % which, l, DFF)
        for t in range(11):
            sl, sl_b = wslot(c, [
                (lambda a: a[:, 0:4096].rearrange("p (k g n) -> p k g n", k=8, g=2)[:, :, 0, :], kx(w_in, t * 256, 256)),
                (lambda a: a[:, 0:4096].rearrange("p (k g n) -> p k g n", k=8, g=2)[:, :, 1, :], kx(w_in, DFF + t * 256, 256)),
            ])
            sv = sl[:, 0:4096].rearrange("p (k g n) -> p k g n", k=8, g=2)
            for hc in range(2):
                hidx = t * 2 + hc
                psg, pbg = S.get_psum()
                psu, pbu = S.get_psum()

                def mmg(e):
                    ins = None
                    for k in range(8):
                        ins = e.matmul(psg[:, :], lhsT=sv[:, k, 0, hc * 128:(hc + 1) * 128], rhs=hb[:, k, :],
                                       start=(k == 0), stop=(k == 7))
                    return ins

                def mmu(e):
                    ins = None
                    for k in range(8):
                        ins = e.matmul(psu[:, :], lhsT=sv[:, k, 1, hc * 128:(hc + 1) * 128], rhs=hb[:, k, :],
                                       start=(k == 0), stop=(k == 7))
                    return ins
                P(mmg, r=[sl_b, hb_b], w=[pbg])
                P(mmu, r=[sl_b, hb_b], w=[pbu])
                sg, sg_b = rr(c, c.sg)
                ru, ru_b = rr(c, c.ru)
                V(lambda e: e.tensor_tensor(out=sg, in0=psg, in1=rs_ap, op=ALU.mult), r=[pbg, rs_b], w=[sg_b])
                A(lambda e: e.activation(out=sg, in_=sg, func=AF.Silu), r=[sg_b], w=[sg_b])
                V(lambda e: e.tensor_tensor(out=ru, in0=psu, in1=rs_ap, op=ALU.mult), r=[pbu, rs_b], w=[ru_b])
                V(lambda e: e.tensor_tensor(out=act[:, hidx, :], in0=sg, in1=ru, op=ALU.mult),
                  r=[sg_b, ru_b], w=[act_b], add_w=(hidx > 0))
        if mid is not None:
            mid()
        pending = None
        for mq in range(4):
            sl, sl_b = wslot(c, [(lambda a: a[:, 0:5632].rearrange("p (k n) -> p k n", k=22),
                                  kx(w_out, mq * 256, 256))])
            sv = sl[:, 0:5632].rearrange("p (k n) -> p k n", k=22)
            for mh in range(2):
                m = mq * 2 + mh
                ps, pb = S.get_psum()

                def mmd(e):
                    ins = None
                    for k in range(22):
                        ins = e.matmul(ps[:, :], lhsT=sv[:, k, mh * 128:(mh + 1) * 128], rhs=act[:, k, :],
                                       start=(k == 0), stop=(k == 21))
                    return ins
                P(mmd, r=[sl_b, act_b], w=[pb])
                if pending is not None:
                    pending()
                    pending = None
                V(lambda e: e.scalar_tensor_tensor(out=x_ap[:, m, :], in0=ps, scalar=0.5, in1=x_ap[:, m, :],
                                                   op0=ALU.mult, op1=ALU.add),
                  r=[pb, x_b[m]], w=[x_b[m]], add_w=True)
                if want_next:
                    pending = norm_hook(c, x_ap, x_b, m, next_g)
        if pending is not None:
            pending()

    def mem_attn(c, s, l, mo_ap, mo_b):
        qmn, qmn_b = c.qmn
        km, km_b = kmT[(s, l)]
        vm, vm_b = Vm[(s, l)]
        c.pt_i = getattr(c, "pt_i", 0)

        def qk(hm):
            fc, hp = hm // 2, hm % 2
            pts = []
            for mt in range(2):
                ps, pb = S.get_psum()
                P(lambda e: e.matmul(
                    ps[:, :], lhsT=km[hp * 64:(hp + 1) * 64, fc, mt * 128:(mt + 1) * 128],
                    rhs=qmn[hp * 64:(hp + 1) * 64, fc, :], start=True, stop=True),
                  r=[km_b, qmn_b], w=[pb])
                pt, pt_b = c.pT[c.pt_i % 4]
                c.pt_i += 1
                A(lambda e: e.activation(out=pt, in_=ps, func=AF.Exp), r=[pb], w=[pt_b])
                pts.append((pt, pt_b))
            return pts

        def pv(hm, pts):
            pso, pbo = S.get_psum()
            psd, pbd = S.get_psum()

            def mmo(e):
                ins = None
                for mt in range(2):
                    ins = e.matmul(pso[0:64, :], lhsT=vm[:, mt, hm * 64:(hm + 1) * 64], rhs=pts[mt][0],
                                   start=(mt == 0), stop=(mt == 1))
                return ins

            def mmden(e):
                ins = None
                for mt in range(2):
                    ins = e.matmul(psd[0:64, :], lhsT=ones_bf[:, 0:64], rhs=pts[mt][0],
                                   start=(mt == 0), stop=(mt == 1))
                return ins
            P(mmo, r=[vm_b, pts[0][1], pts[1][1]], w=[pbo])
            P(mmden, r=[b_const, pts[0][1], pts[1][1]], w=[pbd])
            rec, rec_b = rr(c, c.rec)
            A(lambda e: e.activation(out=rec, in_=psd[0:64, :], func=AF.Ln), r=[pbd], w=[rec_b])
            A(lambda e: e.activation(out=rec, in_=rec, func=AF.Exp, scale=-1.0), r=[rec_b], w=[rec_b])
            V(lambda e: e.tensor_tensor(out=mo_ap[:, hm, :], in0=rec, in1=pso[0:64, :], op=ALU.mult),
              r=[rec_b, pbo], w=[mo_b], add_w=(hm > 0))

        prev = qk(0)
        for hm in range(4):
            nxt = qk(hm + 1) if hm + 1 < 4 else None
            pv(hm, prev)
            prev = nxt

    def mo_project(c, wname, mo_ap, mo_b, x_ap, x_b, extra_k=None, hook=False, next_g=None):
        w = WB[wname]
        slm, slm_b = wslot(c, [(lambda a: a[0:64, 0:4096].rearrange("p (h n) -> p h n", h=4),
                                w[768:1024, :].rearrange("(h p) n -> p h n", p=64))])
        svm = slm[0:64, 0:4096].rearrange("p (h n) -> p h n", h=4)
        pend = [None]
        for half in range(2):
            if extra_k is not None:
                sle, sle_b = wslot(c, [(lambda a: a[:, 0:3072].rearrange("p (k n) -> p k n", k=6),
                                        w[0:768, half * 512:(half + 1) * 512].rearrange("(k p) n -> p k n", p=128))])
                sve = sle[:, 0:3072].rearrange("p (k n) -> p k n", k=6)
            for mm_ in range(4):
                m = half * 4 + mm_
                ps, pb = S.get_psum()

                def mmf(e, ps=ps, m=m, mm_=mm_):
                    ins = None
                    first = True
                    if extra_k is not None:
                        for k in range(6):
                            ins = e.matmul(ps[:, :], lhsT=sve[:, k, mm_ * 128:(mm_ + 1) * 128], rhs=extra_k[0][:, k, :],
                                           start=first, stop=(mo_ap is None and k == 5))
                            first = False
                    if mo_ap is not None:
                        for h in range(4):
                            ins = e.matmul(ps[:, :], lhsT=svm[:, h, m * 128:(m + 1) * 128], rhs=mo_ap[:, h, :],
                                           start=first, stop=(h == 3))
                            first = False
                    return ins
                rds = [slm_b]
                if mo_ap is not None:
                    rds.append(mo_b)
                if extra_k is not None:
                    rds += [sle_b, extra_k[1]]
                P(mmf, r=rds, w=[pb])
                if pend[0] is not None:
                    pend[0]()
                    pend[0] = None
                V(lambda e, ps=ps, m=m: e.tensor_tensor(out=x_ap[:, m, :], in0=ps, in1=x_ap[:, m, :], op=ALU.add),
                  r=[pb, x_b[m]], w=[x_b[m]], add_w=True)
                if hook:
                    pend[0] = norm_hook(c, x_ap, x_b, m, next_g)
        if pend[0] is not None:
            pend[0]()

    def seal(x_b):
        pass

    c = make_ctx("pm")
    ar = c.ar
    memtm = [ar.alloc([128, 2, D], F32, "memtm") for _ in range(2)]
    memT, memT_b = ar.alloc([128, 8, NMEM], F32, "memT")
    memh, memh_b = ar.alloc([128, 8, NMEM], BF16, "memh")
    for l in range(2):
        wkv = WL("w_mem_kv", l, D)
        sl, sl_b = wslot(c, [(lambda a: a[:, 0:4096].rearrange("p (k n) -> p k n", k=8), kx(wkv, 0, 512))])
        sv = sl[:, 0:4096].rearrange("p (k n) -> p k n", k=8)
        for s in range(NSEQ):
            mt_ap, mt_b = memtm[(l * NSEQ + s) % 2]
            S.dma("pool", mt_ap, mem_in[s * NMEM:(s + 1) * NMEM, :].rearrange("(t p) d -> p t d", p=128), writes=[mt_b])
            for t in range(2):
                for half in range(2):
                    ps, pb = S.get_psum()

                    def tr(e, ps=ps, t=t, half=half, mt_ap=mt_ap):
                        ins = None
                        for j in range(4):
                            cc = half * 4 + j
                            ins = e.transpose(out=ps[:, j * 128:(j + 1) * 128], in_=mt_ap[:, t, cc * 128:(cc + 1) * 128],
                                              identity=ident)
                        return ins
                    P(tr, r=[mt_b, b_const], w=[pb])
                    A(lambda e, ps=ps, t=t, half=half: e.copy(
                        out=memT[:, half * 4:(half + 1) * 4, t * 128:(t + 1) * 128],
                        in_=ps.rearrange("p (a b) -> p a b", a=4)), r=[pb], w=[memT_b], add_w=not (t == 0 and half == 0))
            if stop == 'pm1':
                S.emit()
                return nc
            rms(c, memT, memT_b, C_MEM + l * 8, memh, memh_b, n=NMEM)
            if stop == 'pm2':
                S.emit()
                return nc
            km, km_b = kmT[(s, l)]
            vm, vm_b = Vm[(s, l)]
            for fc in range(2):
                ps, pb = S.get_psum()

                def mmk(e, ps=ps, fc=fc):
                    ins = None
                    for k in range(8):
                        ins = e.matmul(ps[:, :NMEM], lhsT=sv[:, k, fc * 128:(fc + 1) * 128], rhs=memh[:, k, :],
                                       start=(k == 0), stop=(k == 7))
                    return ins
                P(mmk, r=[sl_b, memh_b], w=[pb])
                if stop == 'pm3':
                    S.emit()
                    return nc
                headnorm(c, ps, pb, C_MK + l, km[:, fc, :], km_b, n=NMEM, add_w=(fc > 0))
                if stop == 'pm4':
                    S.emit()
                    return nc
            for mt in range(2):
                ps, pb = S.get_psum()

                def mmv(e, ps=ps, mt=mt):
                    ins = None
                    for k in range(8):
                        ins = e.matmul(ps[:, :256], lhsT=memh[:, k, mt * 128:(mt + 1) * 128], rhs=sv[:, k, 256:512],
                                       start=(k == 0), stop=(k == 7))
                    return ins
                P(mmv, r=[sl_b, memh_b], w=[pb])
                A(lambda e, ps=ps, mt=mt, vm=vm: e.copy(out=vm[:, mt, :], in_=ps[:, :256]), r=[pb], w=[vm_b], add_w=(mt > 0))
    S.barrier()

    if stop == 'pm':
        S.emit()
        return nc
    c = make_ctx("pa", nx=2, nmo=1)
    ar = c.ar
    xin = [ar.alloc([128, D], F32, "xin") for _ in range(4)]
    qn, qn_b = ar.alloc([128, 12, CH], BF16, "qn")
    vt, vt_b = ar.alloc([128, 4, 768], BF16, "vt")
    rtm, rtm_b = ar.alloc([128, 4], F32, "rtm")
    w_in_a = WB["w_in_a"]

    def load_x_chunk(ci):
        for j in range(4):
            ap, b = xin[j]
            S.dma("pool", ap, x_in[ci * CH + j * 128: ci * CH + (j + 1) * 128, :], writes=[b])

    def transposes(ci):
        x_ap, x_b = c.xT[ci % 2]
        for j in range(4):
            xi, xi_b = xin[j]
            for half in range(2):
                ps, pb = S.get_psum()

                def tr(e):
                    ins = None
                    for jj in range(4):
                        cc = half * 4 + jj
                        ins = e.transpose(out=ps[:, jj * 128:(jj + 1) * 128], in_=xi[:, cc * 128:(cc + 1) * 128], identity=ident)
                    return ins
                P(tr, r=[xi_b, b_const], w=[pb])
                if half == 0:
                    A(lambda e: e.copy(out=x_ap[:, 0:4, j * 128:(j + 1) * 128], in_=ps.rearrange("p (a b) -> p a b", a=4)),
                      r=[pb], w=[x_b[0:4]], add_w=(j > 0))
                else:
                    V(lambda e: e.tensor_copy(out=x_ap[:, 4:8, j * 128:(j + 1) * 128], in_=ps.rearrange("p (a b) -> p a b", a=4)),
                      r=[pb], w=[x_b[4:8]], add_w=(j > 0))

    def first_hooks_act(ci):
        x_ap, x_b = c.xT[ci % 2]
        return [norm_hook(c, x_ap, x_b, k, C_FFN1, mul_on_dve=True) for k in range(8)]

    def prep_next(ci):
        if ci + 1 < NCHK:
            transposes(ci + 1)
            if ci + 2 < NCHK:
                load_x_chunk(ci + 2)

    load_x_chunk(0)
    transposes(0)
    if NCHK > 1:
        load_x_chunk(1)
    for th in first_hooks_act(0):
        th()
    rs1 = norm_finish(c)
    for ci in range(NCHK):
        s = ci // (SEQ // CH)
        t0 = ci * CH
        x_ap, x_b = c.xT[ci % 2]
        ffn(c, x_ap, x_b, 0, 1, rs1, C_MIX, mid=(lambda ci=ci: prep_next(ci)))
        S.dma("pool", XA[:, :, t0:t0 + CH].rearrange("c p t -> p c t"), x_ap, reads=[x_b], writes=[b_XA], add_w=True)
        hb, hb_b = c.hb
        rs2 = norm_finish(c)
        pst, pbt = S.get_psum()

        def trr(e):
            ins = None
            for j in range(4):
                ins = e.transpose(out=pst[:, j * 128:(j + 1) * 128], in_=rs2[0][:, j * 128:(j + 1) * 128], identity=ident)
            return ins
        P(trr, r=[rs2[1], b_const], w=[pbt])
        V(lambda e: e.tensor_copy(out=rtm, in_=pst.rearrange("p (j q) -> p j q", j=4)[:, :, 0]), r=[pbt], w=[rtm_b])
        first_q = True
        pend_h = [None]
        for t in range(5):
            sl, sl_b = wslot(c, [(lambda a: a[:, 0:4096].rearrange("p (k n) -> p k n", k=8), kx(w_in_a, t * 512, 512))])
            sv = sl[:, 0:4096].rearrange("p (k n) -> p k n", k=8)
            if t == 3 or t == 4:
                nv = 512 if t == 3 else 256
                for j in range(4):
                    ps, pb = S.get_psum()

                    def mmv(e, ps=ps, j=j, sv=sv, nv=nv):
                        ins = None
                        for k in range(8):
                            ins = e.matmul(ps[:, :nv], lhsT=hb[:, k, j * 128:(j + 1) * 128], rhs=sv[:, k, 0:nv],
                                           start=(k == 0), stop=(k == 7))
                        return ins
                    P(mmv, r=[sl_b, hb_b], w=[pb])
                    if pend_h[0] is not None:
                        pend_h[0]()
                        pend_h[0] = None
                    A(lambda e, ps=ps, j=j, nv=nv, t=t: e.mul(out=vt[:, j, (t - 3) * 512:(t - 3) * 512 + nv], in_=ps[:, :nv],
                                                              mul=rtm[:, j:j + 1]),
                      r=[pb, rtm_b], w=[vt_b], add_w=not (t == 3 and j == 0))
            blocks = {0: [0, 1, 2, 3], 1: [0, 1, 2, 3], 2: [0, 1, 2, 3], 3: [], 4: [2, 3]}[t]
            for bi in blocks:
                gcolidx = t * 4 + bi
                ps, pb = S.get_psum()

                def mmq(e, ps=ps, sv=sv, bi=bi):
                    ins = None
                    for k in range(8):
                        ins = e.matmul(ps[:, :], lhsT=sv[:, k, bi * 128:(bi + 1) * 128], rhs=hb[:, k, :],
                                       start=(k == 0), stop=(k == 7))
                    return ins
                P(mmq, r=[sl_b, hb_b], w=[pb])
                if gcolidx < 6:
                    newp = headnorm(c, ps, pb, C_NAQS, qn[:, gcolidx, :], qn_b, add_w=not first_q, rs=rs2, defer=True)
                    first_q = False
                elif gcolidx < 12:
                    newp = headnorm(c, ps, pb, C_NAK, qn[:, gcolidx, :], qn_b, add_w=True, rs=rs2, defer=True)
                else:
                    fcq = gcolidx - 18
                    newp = headnorm(c, ps, pb, C_MQS + 0, c.qmn[0][:, fcq, :], c.qmn[1], add_w=(fcq > 0), rs=rs2, defer=True)
                if pend_h[0] is not None:
                    pend_h[0]()
                pend_h[0] = newp
        if pend_h[0] is not None:
            pend_h[0]()
            pend_h[0] = None
        S.dma("pool", Q_d[:, :, t0:t0 + CH].rearrange("c p t -> p c t"), qn[:, 0:6, :], reads=[qn_b], writes=[b_Q], add_w=True)
        S.dma("pool", K_d[:, :, t0:t0 + CH].rearrange("c p t -> p c t"), qn[:, 6:12, :], reads=[qn_b], writes=[b_K], add_w=True)
        S.dma("pool", V_d[t0:t0 + CH, :].rearrange("(j p) n -> p j n", p=128), vt, reads=[vt_b], writes=[b_V], add_w=True)
        ths = first_hooks_act(ci + 1) if ci + 1 < NCHK else []
        mo, mo_b = c.mo[ci % len(c.mo)]
        mem_attn(c, s, 0, mo, mo_b)
        S.dma("pool", MO_d[:, :, t0:t0 + CH], mo, reads=[mo_b], writes=[b_MO], add_w=True)
        for th in ths:
            th()
        if ths:
            rs1 = norm_finish(c)
    S.barrier()

    if stop == 'pa':
        S.emit()
        return nc
    ar = Arena(nc, PERM_END, "pb")
    TT, TT_b = ar.alloc([128, 12, 14, 64], F32, "TT")
    Qs = [ar.alloc([128, SEQ], BF16, "Qs") for _ in range(2)]
    Ks = [ar.alloc([128, SEQ], BF16, "Ks") for _ in range(2)]
    Ve = [ar.alloc([128, 64, 128], BF16, "Ve") for _ in range(2)]
    Vo = [ar.alloc([128, 63, 128], BF16, "Vo") for _ in range(2)]
    st = [ar.alloc([128, 256], F32, "st") for _ in range(8)]
    pt = [ar.alloc([128, 8, 64], BF16, "pt") for _ in range(5)]
    recb = [ar.alloc([64, 256], F32, "recb") for _ in range(3)]
    ot = [ar.alloc([64, 1024], BF16, "ot") for _ in range(3)]
    V(lambda e: e.memset(TT, -1e30), w=[TT_b])
    for par in range(2):
        for kc in range(64):
            qlo = 0 if kc <= 15 else kc - 7
            qhi = 63 if kc >= 48 else kc + 8
            nq = qhi - qlo + 1
            c0 = 15 - kc + qlo
            p = par * 64 + kc
            S.dma("sp", TT[p:p + 1, :, :, qlo:qhi + 1], rpb_in[:, par:par + 14, c0:c0 + nq].rearrange("(o h) j q -> o h j q", o=1),
                  writes=[TT_b], add_w=not (par == 0 and kc == 0))
    groups = []
    for s in range(NSEQ):
        for fc in range(6):
            groups.append((s, fc))

    def load_group(gi):
        s, fc = groups[gi]
        Qa, Q_b = Qs[gi % 2]
        Ka, K_b = Ks[gi % 2]
        Vea, Ve_b = Ve[gi % 2]
        Voa, Vo_b = Vo[gi % 2]
        tb = s * SEQ
        S.dma("sp", Qa, Q_d[fc, :, tb:tb + SEQ], reads=[b_Q], writes=[Q_b])
        S.dma("sp", Ka, K_d[fc, :, tb:tb + SEQ], reads=[b_K], writes=[K_b])
        S.dma("sp", Vea, V_d[tb:tb + SEQ, fc * 128:(fc + 1) * 128].rearrange("(i p) n -> p i n", p=128),
              reads=[b_V], writes=[Ve_b])
        S.dma("sp", Voa, V_d[tb + 64:tb + 64 + 63 * 128, fc * 128:(fc + 1) * 128].rearrange("(i p) n -> p i n", p=128),
              reads=[b_V], writes=[Vo_b])

    units = []
    for gi, (s, fc) in enumerate(groups):
        for hp in range(2):
            for rg in range(8):
                for r4 in range(4):
                    for rp in range(2):
                        units.append(dict(gi=gi, s=s, fc=fc, hp=hp, rg=rg, r4=r4, rp=rp,
                                          rows=[rg * 16 + r4 * 4 + rp * 2 + i for i in range(2)],
                                          first=(hp == 0 and rg == 0 and r4 == 0 and rp == 0)))
    st_i = [0]
    pt_i = [0]
    ot_i = [0]
    rc_i = [0]
    cur = {}
    TTv = TT.rearrange("p h (a two) q -> p h a two q", two=2)

    def u_qk(u):
        gi, hp = u["gi"], u["hp"]
        Qa, Q_b = Qs[gi % 2]
        Ka, K_b = Ks[gi % 2]
        pl, ph = hp * 64, hp * 64 + 64
        pss, pbs = S.get_psum()
        u["pss"], u["pbs"] = pss, pbs
        rows = u["rows"]

        def mmqk(e):
            ins = None
            for i, r in enumerate(rows):
                rs = min(max(r - 4, 0), 120)
                for t in range(4):
                    k0 = rs * 64 + t * 128
                    ins = e.matmul(pss[:, (i * 4 + t) * 64:(i * 4 + t + 1) * 64],
                                   lhsT=Ka[pl:ph, k0:k0 + 128], rhs=Qa[pl:ph, r * 64:(r + 1) * 64],
                                   start=True, stop=True)
            return ins
        P(mmqk, r=[K_b, Q_b], w=[pbs])

    def u_sm(u):
        h = u["fc"] * 2 + u["hp"]
        pss, pbs = u["pss"], u["pbs"]
        p_ap, p_b = pt[pt_i[0] % len(pt)]
        pt_i[0] += 1
        u["p_ap"], u["p_b"] = p_ap, p_b
        for i, r in enumerate(u["rows"]):
            rs = min(max(r - 4, 0), 120)
            dr0 = rs - r + 7
            s_ap, s_b = st[st_i[0] % len(st)]
            st_i[0] += 1
            V(lambda e: e.tensor_tensor(
                out=s_ap.rearrange("p (t q) -> p t q", t=4),
                in0=pss[:, i * 256:(i + 1) * 256].rearrange("p (t q) -> p t q", t=4),
                in1=TTv[:, h, dr0 // 2:dr0 // 2 + 4, dr0 % 2, :], op=ALU.add),
              r=[pbs, TT_b], w=[s_b])
            A(lambda e: e.activation(out=p_ap[:, i * 4:(i + 1) * 4, :], in_=s_ap.rearrange("p (t q) -> p t q", t=4),
                                     func=AF.Exp),
              r=[s_b], w=[p_b], add_w=(i > 0))

    def u_pv(u):
        gi, hp, rp, r4, rg, fc, s = u["gi"], u["hp"], u["rp"], u["r4"], u["rg"], u["fc"], u["s"]
        Vea, Ve_b = Ve[gi % 2]
        Voa, Vo_b = Vo[gi % 2]
        pl, ph = hp * 64, hp * 64 + 64
        if rp == 0:
            cur["pso"], cur["pbo"] = S.get_psum()
        pso, pbo = cur["pso"], cur["pbo"]
        p_ap, p_b = u["p_ap"], u["p_b"]
        rows = u["rows"]

        def mmpv(e):
            ins = None
            for i, r in enumerate(rows):
                rs = min(max(r - 4, 0), 120)
                col = (rp * 2 + i) * 64
                for t in range(4):
                    if rs % 2 == 0:
                        vl = Vea[:, rs // 2 + t, pl:ph]
                    else:
                        vl = Voa[:, (rs - 1) // 2 + t, pl:ph]
                    ins = e.matmul(pso[0:64, col:col + 64], lhsT=vl, rhs=p_ap[:, i * 4 + t, :],
                                   start=(t == 0), stop=(t == 3))
                for t in range(4):
                    ins = e.matmul(pso[0:64, 256 + col:256 + col + 64], lhsT=ones_bf[:, 0:64],
                                   rhs=p_ap[:, i * 4 + t, :], start=(t == 0), stop=(t == 3))
            return ins
        P(mmpv, r=[p_b, Ve_b, Vo_b, b_const], w=[pbo], add_w=(rp > 0))
        u["pso"], u["pbo"] = pso, pbo

    def u_norm(u):
        hp, rp, r4, rg, fc, s = u["hp"], u["rp"], u["r4"], u["rg"], u["fc"], u["s"]
        pl, ph = hp * 64, hp * 64 + 64
        pso, pbo = u["pso"], u["pbo"]
        if rp == 1:
            if r4 == 0:
                cur["ot"] = ot[ot_i[0] % len(ot)]
                ot_i[0] += 1
            o_ap, o_b = cur["ot"]
            rc, rc_b = recb[rc_i[0] % len(recb)]
            rc_i[0] += 1
            A(lambda e: e.activation(out=rc, in_=pso[0:64, 256:512], func=AF.Ln), r=[pbo], w=[rc_b])
            A(lambda e: e.activation(out=rc, in_=rc, func=AF.Exp, scale=-1.0), r=[rc_b], w=[rc_b])
            V(lambda e: e.tensor_tensor(out=o_ap[:, r4 * 256:(r4 + 1) * 256], in0=rc, in1=pso[0:64, 0:256], op=ALU.mult),
              r=[rc_b, pbo], w=[o_b], add_w=(r4 > 0))
            if r4 == 3:
                tb = s * SEQ
                S.dma("pool", O_d[fc, pl:ph, tb + rg * 1024: tb + (rg + 1) * 1024], o_ap, reads=[o_b], writes=[b_O], add_w=True)

    DEPTH = 2
    NLAG = 2
    load_group(0)
    for i in range(len(units) + DEPTH + NLAG):
        if i < len(units):
            u_qk(units[i])
        if DEPTH <= i < len(units) + DEPTH:
            u = units[i - DEPTH]
            if u["first"] and u["gi"] + 1 < len(groups):
                load_group(u["gi"] + 1)
            u_sm(u)
            u_pv(u)
        if i >= DEPTH + NLAG:
            u_norm(units[i - DEPTH - NLAG])
    S.barrier()

    if stop == 'pb':
        S.emit()
        return nc
    c = make_ctx("pc")
    ar = c.ar
    oin = [ar.alloc([128, 6, CH], BF16, "oin") for _ in range(2)]
    moin = [ar.alloc([64, 4, CH], BF16, "moin") for _ in range(2)]
    zT, zT_b = ar.alloc([128, 6, CH], BF16, "zT")
    abt = [ar.alloc([128, 2, 768], BF16, "abt") for _ in range(2)]
    w_in_b = WB["w_in_b"]

    def load_c(ci):
        t0 = ci * CH
        x_ap, x_b = c.xT[ci % 2]
        S.dma("pool", x_ap, XA[:, :, t0:t0 + CH].rearrange("c p t -> p c t"), reads=[b_XA], writes=[x_b])
        S.dma("pool", oin[ci % 2][0], O_d[:, :, t0:t0 + CH].rearrange("c p t -> p c t"), reads=[b_O], writes=[oin[ci % 2][1]])
        S.dma("pool", moin[ci % 2][0], MO_d[:, :, t0:t0 + CH], reads=[b_MO], writes=[moin[ci % 2][1]])

    b_XA2 = Buf("XA2")
    load_c(0)
    for ci in range(NCHK):
        s = ci // (SEQ // CH)
        t0 = ci * CH
        x_ap, x_b = c.xT[ci % 2]
        if ci + 1 < NCHK:
            load_c(ci + 1)
        mo_project(c, "w_out_a", moin[ci % 2][0], moin[ci % 2][1], x_ap, x_b, extra_k=oin[ci % 2], hook=True, next_g=C_FFN2)
        rsA = norm_finish(c)
        ffn(c, x_ap, x_b, 0, 2, rsA, None)
        rsB = norm_finish(c)
        for k in range(8):
            V(lambda e: e.scalar_tensor_tensor(out=x_ap[:, k, :], in0=x_ap[:, k, :], scalar=cols[:, C_OUT + k:C_OUT + k + 1],
                                               in1=rsB[0], op0=ALU.mult, op1=ALU.mult),
              r=[x_b[k], rsB[1], b_cols], w=[x_b[k]], add_w=True)
            norm_hook(c, x_ap, x_b, k, C_FFN1 + 8)()
        rsC = norm_finish(c)
        ffn(c, x_ap, x_b, 1, 1, rsC, C_MIX + 8)
        hb, hb_b = c.hb
        rsD = norm_finish(c)
        pend_h = [None]
        for t in range(2):
            sl, sl_b = wslot(c, [(lambda a: a[:, 0:4096].rearrange("p (k n) -> p k n", k=8), kx(w_in_b, t * 512, 512))])
            sv = sl[:, 0:4096].rearrange("p (k n) -> p k n", k=8)
            for bi in range(4):
                g = t * 4 + bi
                ps, pb = S.get_psum()

                def mmz(e, ps=ps, sv=sv, bi=bi):
                    ins = None
                    for k in range(8):
                        ins = e.matmul(ps[:, :], lhsT=sv[:, k, bi * 128:(bi + 1) * 128], rhs=hb[:, k, :],
                                       start=(k == 0), stop=(k == 7))
                    return ins
                P(mmz, r=[sl_b, hb_b], w=[pb])
                if g < 6 and pend_h[0] is not None:
                    pend_h[0]()
                    pend_h[0] = None
                if g < 6:
                    V(lambda e, ps=ps, g=g: e.tensor_tensor(out=zT[:, g, :], in0=ps, in1=rsD[0], op=ALU.mult),
                      r=[pb, rsD[1]], w=[zT_b], add_w=(g > 0))
                else:
                    newp = headnorm(c, ps, pb, C_MQS + 1, c.qmn[0][:, g - 6, :], c.qmn[1], add_w=(g > 6), rs=rsD, defer=True)
                    if pend_h[0] is not None:
                        pend_h[0]()
                    pend_h[0] = newp
        if pend_h[0] is not None:
            pend_h[0]()
            pend_h[0] = None
        mo, mo_b = c.mo[ci % 2]
        mem_attn(c, s, 1, mo, mo_b)
        mo_project(c, "w_out_b", mo, mo_b, x_ap, x_b, extra_k=None)
        S.dma("pool", XA[:, :, t0:t0 + CH].rearrange("c p t -> p c t"), x_ap, reads=[x_b], writes=[b_XA2], add_w=True)
        dsl = []
        for src_in in (cd_in, sd_in):
            ap_, b_ = wslot(c, [(lambda a: a[:, 0:4608].rearrange("p (k n) -> p k n", k=6),
                                 src_in.rearrange("(k p) n -> p k n", p=128))])
            dsl.append((ap_[:, 0:4608].rearrange("p (k n) -> p k n", k=6), b_))
        for j in range(4):
            ab, ab_b = abt[j % 2]
            for wi, (Ws, W_b) in enumerate(dsl):
                for half in range(2):
                    ps, pb = S.get_psum()

                    def mma(e, ps=ps, j=j, Ws=Ws, half=half):
                        ins = None
                        for k in range(6):
                            ins = e.matmul(ps[:, :384], lhsT=zT[:, k, j * 128:(j + 1) * 128],
                                           rhs=Ws[:, k, half * 384:(half + 1) * 384], start=(k == 0), stop=(k == 5))
                        return ins
                    P(mma, r=[zT_b, W_b], w=[pb])
                    if half == 0:
                        A(lambda e, ps=ps, ab=ab, wi=wi, half=half: e.copy(out=ab[:, wi, half * 384:(half + 1) * 384], in_=ps[:, :384]),
                          r=[pb], w=[ab_b], add_w=not (wi == 0 and half == 0))
                    else:
                        V(lambda e, ps=ps, ab=ab, wi=wi, half=half: e.tensor_copy(out=ab[:, wi, half * 384:(half + 1) * 384], in_=ps[:, :384]),
                          r=[pb], w=[ab_b], add_w=True)
            r0 = (t0 - s * SEQ + j * 128) // 64
            S.dma("pool", A_d[s, :, r0:r0 + 2, :, :].rearrange("n a b m -> (a b) n m"),
                  ab[:, 0, :].rearrange("p (n m) -> p n m", n=6), reads=[ab_b], writes=[b_A], add_w=True)
            S.dma("pool", B_d[s, :, r0:r0 + 2, :, :].rearrange("n a b m -> (a b) n m"),
                  ab[:, 1, :].rearrange("p (n m) -> p n m", n=6), reads=[ab_b], writes=[b_B], add_w=True)
    S.barrier()

    if stop == 'pc':
        S.emit()
        return nc
    ar = Arena(nc, PERM_END, "pd")
    c128, c128_b = ar.alloc([128, 3, 128], BF16, "c128")
    Et, E_b = ar.alloc([64, 2, SEQ], BF16, "E")
    ATs = [ar.alloc([128, 64, 128], BF16, "AT") for _ in range(2)]
    BTs = [ar.alloc([128, 64, 128], BF16, "BT") for _ in range(2)]
    Yc, Yc_b = ar.alloc([64, 2, 64, 128], BF16, "Yc")
    FTt = [ar.alloc([64, SEQ], BF16, "FTt") for _ in range(2)]
    S.dma("sp", c128, c128_in, writes=[c128_b])
    S.dma("sp", Et, e_in, writes=[E_b])
    kk = 0
    for s in range(NSEQ):
        for nb in range(6):
            AT, AT_b = ATs[kk % 2]
            BT, BT_b = BTs[kk % 2]
            kk += 1
            S.dma("sp", AT, A_d[s, nb], reads=[b_A], writes=[AT_b])
            S.dma("sp", BT, B_d[s, nb], reads=[b_B], writes=[BT_b])
            for mh in range(2):
                for m4 in range(16):
                    psr, pbr = S.get_psum()
                    psi, pbi = S.get_psum()

                    def mm1(e):
                        ins = None
                        for mm_ in range(4):
                            m = mh * 64 + m4 * 4 + mm_
                            o = mm_ * 128
                            e.matmul(psr[0:64, o:o + 128], lhsT=AT[:, :, m], rhs=c128[:, 0, :], start=True, stop=False)
                            e.matmul(psr[0:64, o:o + 128], lhsT=BT[:, :, m], rhs=c128[:, 1, :], start=False, stop=True)
                            e.matmul(psi[0:64, o:o + 128], lhsT=AT[:, :, m], rhs=c128[:, 1, :], start=True, stop=False)
                            ins = e.matmul(psi[0:64, o:o + 128], lhsT=BT[:, :, m], rhs=c128[:, 2, :], start=False, stop=True)
                        return ins
                    P(mm1, r=[AT_b, BT_b, c128_b], w=[pbr, pbi])
                    A(lambda e: e.copy(out=Yc[:, 0, m4 * 4:(m4 + 1) * 4, :], in_=psr[0:64, :].rearrange("p (m k) -> p m k", m=4)),
                      r=[pbr], w=[Yc_b], add_w=(m4 > 0))
                    V(lambda e: e.tensor_copy(out=Yc[:, 1, m4 * 4:(m4 + 1) * 4, :], in_=psi[0:64, :].rearrange("p (m k) -> p m k", m=4)),
                      r=[pbi], w=[Yc_b], add_w=True)
                F_ap, F_b = FTt[(kk * 2 + mh) % 2]
                Ev = Et.rearrange("p r (k2 k1) -> p r k1 k2", k1=128)
                for kg in range(16):
                    ps, pb = S.get_psum()

                    def mm3(e):
                        ins = None
                        for kl in range(8):
                            k1 = kg * 8 + kl
                            e.matmul(ps[0:64, kl * 64:(kl + 1) * 64], lhsT=Yc[:, 0, :, k1], rhs=Ev[:, 0, k1, :], start=True, stop=False)
                            ins = e.matmul(ps[0:64, kl * 64:(kl + 1) * 64], lhsT=Yc[:, 1, :, k1], rhs=Ev[:, 1, k1, :], start=False, stop=True)
                        return ins
                    P(mm3, r=[Yc_b, E_b], w=[pb])
                    Fo = F_ap.rearrange("p (k2 k1) -> p k1 k2", k1=128)[:, kg * 8:(kg + 1) * 8, :]
                    if kg % 2 == 0:
                        V(lambda e: e.tensor_copy(out=Fo, in_=ps[0:64, :].rearrange("p (a b) -> p a b", a=8)),
                          r=[pb], w=[F_b], add_w=(kg > 0))
                    else:
                        A(lambda e: e.copy(out=Fo, in_=ps[0:64, :].rearrange("p (a b) -> p a b", a=8)),
                          r=[pb], w=[F_b], add_w=True)
                S.dma("pool", FT_d[nb, mh * 64:(mh + 1) * 64, s * SEQ:(s + 1) * SEQ], F_ap, reads=[F_b], writes=[b_FT], add_w=True)
    S.barrier()

    if stop == 'pd':
        S.emit()
        return nc
    c = make_ctx("pe")
    ar = c.ar
    fin = [ar.alloc([128, 6, CH], BF16, "fin") for _ in range(2)]
    yt = [ar.alloc([128, D], F32, "yt") for _ in range(2)]
    grow, grow_b = ar.alloc([128, D], F32, "grow")
    rtmE, rtmE_b = ar.alloc([128, 4], F32, "rtmE")
    S.dma("sp", grow, grow_in, writes=[grow_b])

    def load_e(ci):
        t0 = ci * CH
        x_ap, x_b = c.xT[ci % 2]
        S.dma("pool", x_ap, XA[:, :, t0:t0 + CH].rearrange("c p t -> p c t"), reads=[b_XA2], writes=[x_b])
        S.dma("pool", fin[ci % 2][0], FT_d[:, :, t0:t0 + CH].rearrange("c p t -> p c t"), reads=[b_FT], writes=[fin[ci % 2][1]])

    load_e(0)
    for ci in range(NCHK):
        t0 = ci * CH
        x_ap, x_b = c.xT[ci % 2]
        if ci + 1 < NCHK:
            load_e(ci + 1)
        mo_project(c, "w_out_b", None, None, x_ap, x_b, extra_k=fin[ci % 2], hook=True, next_g=C_FFN2 + 8)
        rsE = norm_finish(c)
        ffn(c, x_ap, x_b, 1, 2, rsE, None)
        rsF = norm_finish(c)
        pst, pbt = S.get_psum()

        def trr(e):
            ins = None
            for j in range(4):
                ins = e.transpose(out=pst[:, j * 128:(j + 1) * 128], in_=rsF[0][:, j * 128:(j + 1) * 128], identity=ident)
            return ins
        for j in range(4):
            y_ap, y_b = yt[j % 2]
            pss_ = []
            for half in range(2):
                ps, pb = S.get_psum()

                def trb(e):
                    ins = None
                    for jj in range(4):
                        cc = half * 4 + jj
                        ins = e.transpose(out=ps[:, jj * 128:(jj + 1) * 128], in_=x_ap[:, cc, j * 128:(j + 1) * 128], identity=ident)
                    return ins
                P(trb, r=[x_b, b_const], w=[pb])
                pss_.append((ps, pb))
            if j == 0:
                P(trr, r=[rsF[1], b_const], w=[pbt])
                V(lambda e: e.tensor_copy(out=rtmE, in_=pst.rearrange("p (j q) -> p j q", j=4)[:, :, 0]), r=[pbt], w=[rtmE_b])
            for half in range(2):
                ps, pb = pss_[half]
                V(lambda e: e.scalar_tensor_tensor(out=y_ap[:, half * 512:(half + 1) * 512], in0=ps, scalar=rtmE[:, j:j + 1],
                                                   in1=grow[:, half * 512:(half + 1) * 512], op0=ALU.mult, op1=ALU.mult),
                  r=[pb, rtmE_b, grow_b], w=[y_b], add_w=(half > 0))
            S.dma("pool", y_out[t0 + j * 128:t0 + (j + 1) * 128, :], y_ap, reads=[y_b])

    S.emit()
    return nc


def _consts():
    bf = ml_dtypes.bfloat16
    n = np.arange(768)
    ang = 2 * np.pi * np.outer(n, n) / 768.0
    cd = (np.cos(ang) / np.sqrt(768.0)).astype(bf)
    sd = (np.sin(ang) / np.sqrt(768.0)).astype(bf)
    t = np.arange(128)
    a = 2 * np.pi * np.outer(t, t) / 128.0
    c128 = np.stack([np.cos(a), -np.sin(a), -np.cos(a)], axis=1) / np.sqrt(128.0)
    t2 = np.arange(64)
    k = np.arange(SEQ)
    ae = 2 * np.pi * np.outer(t2, k) / float(SEQ)
    e = np.stack([np.cos(ae), np.sin(ae)], axis=1) / np.sqrt(64.0)
    return cd, sd, c128.astype(bf), e.astype(bf)


_CACHE = {}


def _prep_common(inputs):
    f = lambda a: np.ascontiguousarray(np.asarray(a, dtype=np.float32))
    cols = np.zeros((128, NCOL), np.float32)
    for base, name in ((C_FFN1, "norm_ffn1"), (C_MIX, "norm_mix"), (C_MEM, "norm_mem"),
                       (C_FFN2, "norm_ffn2"), (C_OUT, "norm_out")):
        g = f(inputs[name])
        cols[:, base:base + 16] = g.reshape(2, 8, 128).transpose(2, 0, 1).reshape(128, 16)
    for base, name, nl in ((C_MQ, "mem_q_norm", 2), (C_MK, "mem_k_norm", 2), (C_NAQ, "na_q_norm", 1), (C_NAK, "na_k_norm", 1)):
        g = f(inputs[name])
        cols[:, base:base + nl] = np.tile(g.T, (2, 1))
    cd, sd, c128, e = _consts()
    common = {
        "cols_in": cols,
        "rpb_rev": np.ascontiguousarray(f(inputs["na_rpb"])[0][:, :, ::-1]),
        "grow_in": np.ascontiguousarray(np.broadcast_to(f(inputs["norm_out"])[1][None, :], (128, D))),
        "cd_in": cd, "sd_in": sd, "c128_in": c128, "e_in": e,
    }
    for (n, r, c) in WSPEC:
        common[n] = f(inputs[n]).reshape(r, c)
    return common


def kernel(**inputs):
    NSEQ = 2
    if "nc" not in _CACHE:
        _CACHE["nc"] = build_program(NSEQ)
    nc = _CACHE["nc"]
    common = _prep_common(inputs)
    xp = np.asarray(inputs["x_prompt"], dtype=np.float32)
    xs = np.asarray(inputs["x_sample"], dtype=np.float32)
    mp = np.asarray(inputs["mem_prompt"], dtype=np.float32)
    ms = np.asarray(inputs["mem_sample"], dtype=np.float32)
    in_maps = []
    for c in range(8):
        d = dict(common)
        d["x_in"] = np.ascontiguousarray(np.concatenate([xs[c], xp[c % 4]], axis=0))
        d["mem_in"] = np.ascontiguousarray(np.concatenate([ms[c], mp[c % 4]], axis=0))
        in_maps.append(d)
    res = run_bass_kernel_spmd(nc, in_maps, core_ids=list(range(8)))
    y_s = np.stack([res.results[c]["y_out"][:SEQ] for c in range(8)], axis=0)
    y_p = np.stack([res.results[c]["y_out"][SEQ:] for c in range(4)], axis=0)
    return (y_p.astype(np.float32), y_s.astype(np.float32))
```
